# Optimizing a Trainium2 kernel written in Bass

```python
import math
import jax, jax.numpy as jnp
from jax import lax
import numpy as np

D_MODEL = 1024
BATCH = 4
SEQ = 8192
DEPTH = 2

N_EVEN = (DEPTH + 1) // 2
N_ODD = DEPTH // 2
D_FF = 2816
EPS = 1e-6
ROPE_THETA = 500000.0
HEAD_DIM = 64
ROT_DIM = HEAD_DIM // 4
Q_BLOCK = 128
NEG = -1e30
BIG = 1e30

LRU_WIDTH = D_MODEL // 2
LRU_BLOCKS = 8
LRU_BLOCK_DIM = LRU_WIDTH // LRU_BLOCKS
CONV_WIDTH = 4
LRU_C = 8.0
MOBA_HEADS = (D_MODEL // 2) // HEAD_DIM
MOBA_WIDTH = MOBA_HEADS * HEAD_DIM
MOBA_BLOCK = 256
MOBA_TOPK = 3
IN0_WIDTH = 2 * LRU_WIDTH + 3 * MOBA_WIDTH
MIX0_WIDTH = LRU_WIDTH + MOBA_WIDTH

NSA_HEADS = D_MODEL // HEAD_DIM
NSA_KV_GROUPS = 2
NSA_HPG = NSA_HEADS // NSA_KV_GROUPS
CMP_LEN = 32
CMP_STRIDE = 16
SEL_BLOCK = 64
SEL_TOPK = 16
WINDOW = 512
CMP_HIDDEN = 128
NSA_QD = NSA_HEADS * HEAD_DIM
NSA_KVD = NSA_KV_GROUPS * HEAD_DIM
IN1_WIDTH = NSA_QD + 6 * NSA_KVD + 3 * NSA_HEADS
MIX1_WIDTH = NSA_QD

kernel_name = "hybrid_lru_moba_nsa_macaron_adaln"


def rmsnorm(x, g):
    xf = x.astype(jnp.float32)
    y = xf * lax.rsqrt(jnp.mean(xf * xf, axis=-1, keepdims=True) + EPS)
    return (y * g.astype(jnp.float32)).astype(x.dtype)


def modulate(h, shift, scale):
    return h * (1.0 + scale[:, None, :]) + shift[:, None, :]


def swiglu(h, w1, w2):
    a, b = jnp.split(h @ w1, 2, axis=-1)
    return (jax.nn.silu(a) * b) @ w2


def rope_partial(x, pos):
    half = ROT_DIM // 2
    freqs = ROPE_THETA ** (-jnp.arange(half, dtype=jnp.float32) * 2.0 / ROT_DIM)
    ang = pos[:, None] * freqs[None, :]
    cos = jnp.cos(ang).astype(x.dtype)
    sin = jnp.sin(ang).astype(x.dtype)
    x1 = x[..., :half]
    x2 = x[..., half:ROT_DIM]
    return jnp.concatenate([x1 * cos - x2 * sin, x2 * cos + x1 * sin, x[..., ROT_DIM:]], axis=-1)


def masked_softmax(logits, mask):
    p = jax.nn.softmax(jnp.where(mask, logits, NEG), axis=-1)
    return jnp.where(mask, p, 0.0)


def causal_depthwise_conv(x, w, b):
    y = lax.conv_general_dilated(x, w[:, None, :], window_strides=(1,),
                                 padding=((CONV_WIDTH - 1, 0),),
                                 dimension_numbers=('NWC', 'WIO', 'NWC'),
                                 feature_group_count=x.shape[-1])
    return y + b


def rg_lru(x, wa, ba, wx, bx, lam):
    B, S, C = x.shape
    xb = x.reshape(B, S, LRU_BLOCKS, LRU_BLOCK_DIM)
    r = jax.nn.sigmoid(jnp.einsum('bsni,nij->bsnj', xb, wa) + ba).reshape(B, S, C)
    i = jax.nn.sigmoid(jnp.einsum('bsni,nij->bsnj', xb, wx) + bx).reshape(B, S, C)
    log_a = -LRU_C * r.astype(jnp.float32) * jax.nn.softplus(-lam.astype(jnp.float32))
    a = jnp.exp(log_a)
    mult = jnp.sqrt(-jnp.expm1(2.0 * log_a))
    bterm = mult * (i * x).astype(jnp.float32)

    def combine(left, right):
        a_l, b_l = left
        a_r, b_r = right
        return a_l * a_r, a_r * b_l + b_r

    _, h = lax.associative_scan(combine, (a, bterm), axis=1)
    return h.astype(x.dtype)


def moba_attention(q, k, v):
    B, H, S, dh = q.shape
    scale = dh ** -0.5
    nb = -(-S // MOBA_BLOCK)
    pad = nb * MOBA_BLOCK - S
    kp = jnp.pad(k, ((0, 0), (0, 0), (0, pad), (0, 0))).reshape(B, H, nb, MOBA_BLOCK, dh)
    vp = jnp.pad(v, ((0, 0), (0, 0), (0, pad), (0, 0))).reshape(B, H, nb, MOBA_BLOCK, dh)
    centroid = jnp.mean(kp.astype(jnp.float32), axis=3)
    t = jnp.arange(S)
    qblk = t // MOBA_BLOCK
    gate = jnp.einsum('bhsd,bhnd->bhsn', q.astype(jnp.float32), centroid)
    past = jnp.arange(nb)[None, :] < qblk[:, None]
    gate = jnp.where(past, gate, -jnp.inf)
    _, top_idx = lax.top_k(gate, min(MOBA_TOPK, nb))
    own = jnp.broadcast_to(qblk[:, None], (B, H, S, 1)).astype(top_idx.dtype)
    sel_idx = jnp.concatenate([top_idx, own], axis=-1)
    sel_valid = jnp.concatenate([top_idx < qblk[:, None], jnp.ones((B, H, S, 1), bool)], axis=-1)

    nq = S // Q_BLOCK

    def to_blocks(a):
        return jnp.moveaxis(a.reshape(B, H, nq, Q_BLOCK, *a.shape[3:]), 2, 0)

    b_ix = jnp.arange(B)[:, None, None, None]
    h_ix = jnp.arange(H)[None, :, None, None]
    offs = jnp.arange(MOBA_BLOCK)

    def step(args):
        qb, idx, valid, t0 = args
        kg = kp[b_ix, h_ix, idx]
        vg = vp[b_ix, h_ix, idx]
        s = jnp.einsum('bhqd,bhqnkd->bhqnk', qb, kg).astype(jnp.float32) * scale
        tq = t0 + jnp.arange(Q_BLOCK)
        kpos = idx[..., None] * MOBA_BLOCK + offs
        mask = valid[..., None] & (kpos <= tq[:, None, None])
        p = masked_softmax(s.reshape(B, H, Q_BLOCK, -1), mask.reshape(B, H, Q_BLOCK, -1)).reshape(s.shape)
        return jnp.einsum('bhqnk,bhqnkd->bhqd', p.astype(v.dtype), vg)

    out = lax.map(step, (to_blocks(q), to_blocks(sel_idx), to_blocks(sel_valid), jnp.arange(nq) * Q_BLOCK))
    return jnp.moveaxis(out, 0, 2).reshape(B, H, S, dh)


def mixer_lru_moba(h, in_w, conv_w, conv_b, wa, ba, wx, bx, lam, out_w, pos):
    B, S, _ = h.shape
    u = h @ in_w
    o1 = LRU_WIDTH
    o2 = 2 * LRU_WIDTH
    x_lru, g_lru, q, k, v = jnp.split(u, [o1, o2, o2 + MOBA_WIDTH, o2 + 2 * MOBA_WIDTH], axis=-1)
    xc = causal_depthwise_conv(x_lru, conv_w, conv_b)
    y_lru = rg_lru(xc, wa, ba, wx, bx, lam) * jax.nn.gelu(g_lru)

    def heads(a):
        return a.reshape(B, S, MOBA_HEADS, HEAD_DIM).transpose(0, 2, 1, 3)

    q = rope_partial(heads(q), pos)
    k = rope_partial(heads(k), pos)
    y_att = moba_attention(q, k, heads(v)).transpose(0, 2, 1, 3).reshape(B, S, MOBA_WIDTH)
    return jnp.concatenate([y_lru, y_att], axis=-1) @ out_w


def compress(kv, pos_emb, w1, w2):
    B, G, S, dh = kv.shape
    c = kv.reshape(B, G, S // CMP_STRIDE, CMP_STRIDE, dh)
    blocks = jnp.concatenate([c[:, :, :-1], c[:, :, 1:]], axis=3)
    nc = blocks.shape[2]
    blocks = (blocks + pos_emb).reshape(B, G, nc, CMP_LEN * dh)
    return jax.nn.gelu(blocks @ w1) @ w2


def mixer_nsa(h, in_w, cmp_pos, cmp_w1, cmp_w2, out_w, pos):
    B, S, _ = h.shape
    G, R, dh = NSA_KV_GROUPS, NSA_HPG, HEAD_DIM
    scale = dh ** -0.5
    u = h @ in_w
    cuts = [NSA_QD + i * NSA_KVD for i in range(7)]
    q, kc, vc, ks, vs, kw, vw, gl = jnp.split(u, cuts, axis=-1)
    q = q.reshape(B, S, G, R, dh).transpose(0, 2, 3, 1, 4)

    def kvh(a):
        return a.reshape(B, S, G, dh).transpose(0, 2, 1, 3)

    q_rope = rope_partial(q, pos)
    kcmp = compress(kvh(kc), cmp_pos[0], cmp_w1[0], cmp_w2[0])
    vcmp = compress(kvh(vc), cmp_pos[1], cmp_w1[1], cmp_w2[1])
    nc = kcmp.shape[2]
    ns = S // SEL_BLOCK
    ks_blocks = rope_partial(kvh(ks), pos).reshape(B, G, ns, SEL_BLOCK, dh)
    vs_blocks = kvh(vs).reshape(B, G, ns, SEL_BLOCK, dh)
    kw_pad = jnp.pad(rope_partial(kvh(kw), pos), ((0, 0), (0, 0), (WINDOW, 0), (0, 0)))
    vw_pad = jnp.pad(kvh(vw), ((0, 0), (0, 0), (WINDOW, 0), (0, 0)))
    gates = jax.nn.sigmoid(gl).reshape(B, S, NSA_HEADS, 3)

    cmp_end = jnp.arange(nc) * CMP_STRIDE + CMP_LEN - 1
    ci = jnp.arange(nc)[:, None] * CMP_STRIDE
    sj = jnp.arange(ns)[None, :] * SEL_BLOCK
    overlap = jnp.clip(jnp.minimum(ci + CMP_LEN, sj + SEL_BLOCK) - jnp.maximum(ci, sj), 0, None).astype(jnp.float32) / CMP_LEN
    n_sel = min(SEL_TOPK, ns)
    b_ix = jnp.arange(B)[:, None, None, None]
    g_ix = jnp.arange(G)[None, :, None, None]
    offs = jnp.arange(SEL_BLOCK)
    jn = jnp.arange(ns)
    nq = S // Q_BLOCK

    def to_blocks(a):
        return jnp.moveaxis(a.reshape(B, G, R, nq, Q_BLOCK, dh), 3, 0)

    def step(args):
        qn, qr, t0 = args
        tq = t0 + jnp.arange(Q_BLOCK)
        s_c = jnp.einsum('bgrqd,bgcd->bgrqc', qn, kcmp).astype(jnp.float32) * scale
        p_c = masked_softmax(s_c, cmp_end[None, :] <= tq[:, None])
        o_c = jnp.einsum('bgrqc,bgcd->bgrqd', p_c.astype(vcmp.dtype), vcmp)
        imp = jnp.einsum('bgrqc,cn->bgqn', p_c, overlap)
        blk_t = tq // SEL_BLOCK
        forced = (jn[None, :] == 0) | (jn[None, :] == blk_t[:, None]) | (jn[None, :] == blk_t[:, None] - 1)
        imp = jnp.where(forced, BIG, imp)
        imp = jnp.where(jn[None, :] <= blk_t[:, None], imp, -jnp.inf)
        _, idx = lax.top_k(imp, n_sel)
        kg = ks_blocks[b_ix, g_ix, idx]
        vg = vs_blocks[b_ix, g_ix, idx]
        s_s = jnp.einsum('bgrqd,bgqnkd->bgrqnk', qr, kg).astype(jnp.float32) * scale
        kpos = idx[..., None] * SEL_BLOCK + offs
        m_s = (kpos <= tq[:, None, None])[:, :, None]
        p_s = masked_softmax(s_s.reshape(B, G, R, Q_BLOCK, -1), m_s.reshape(B, G, 1, Q_BLOCK, -1)).reshape(s_s.shape)
        o_s = jnp.einsum('bgrqnk,bgqnkd->bgrqd', p_s.astype(vg.dtype), vg)
        kwb = lax.dynamic_slice_in_dim(kw_pad, t0, Q_BLOCK + WINDOW, axis=2)
        vwb = lax.dynamic_slice_in_dim(vw_pad, t0, Q_BLOCK + WINDOW, axis=2)
        s_w = jnp.einsum('bgrqd,bgkd->bgrqk', qr, kwb).astype(jnp.float32) * scale
        kpos_w = t0 - WINDOW + jnp.arange(Q_BLOCK + WINDOW)
        diff = tq[:, None] - kpos_w[None, :]
        m_w = (diff >= 0) & (diff < WINDOW) & (kpos_w[None, :] >= 0)
        p_w = masked_softmax(s_w, m_w)
        o_w = jnp.einsum('bgrqk,bgkd->bgrqd', p_w.astype(vwb.dtype), vwb)
        return jnp.stack([o_c, o_s, o_w], axis=-2)

    out = lax.map(step, (to_blocks(q), to_blocks(q_rope), jnp.arange(nq) * Q_BLOCK))
    out = out.transpose(1, 0, 4, 2, 3, 5, 6).reshape(B, S, NSA_HEADS, 3, dh)
    y = jnp.einsum('bshc,bshcd->bshd', gates, out).reshape(B, S, MIX1_WIDTH)
    return y @ out_w


def setup_inputs(seed: int = 0) -> dict:
    key = jax.random.key(seed)
    ks = jax.random.split(key, 24)
    f32 = jnp.float32
    D = D_MODEL

    def nrm(k, shape, s):
        return s * jax.random.normal(k, shape, f32)

    a8 = jax.random.uniform(ks[14], (N_EVEN, LRU_WIDTH), f32, 0.9, 0.999)
    sig = a8 ** (1.0 / LRU_C)
    return {
        "x": nrm(ks[0], (BATCH, SEQ, D), 1.0),
        "c": nrm(ks[1], (BATCH, D), 1.0),
        "mod_w": nrm(ks[2], (DEPTH, D, 9 * D), 0.5 * D ** -0.5),
        "mod_b": nrm(ks[3], (DEPTH, 9 * D), 0.02),
        "norm_g": 1.0 + nrm(ks[4], (DEPTH, 3, D), 0.05),
        "ffn_w1": nrm(ks[5], (DEPTH, 2, D, 2 * D_FF), D ** -0.5),
        "ffn_w2": nrm(ks[6], (DEPTH, 2, D_FF, D), D_FF ** -0.5),
        "mix0_in_w": nrm(ks[7], (N_EVEN, D, IN0_WIDTH), D ** -0.5),
        "lru_conv_w": nrm(ks[8], (N_EVEN, CONV_WIDTH, LRU_WIDTH), CONV_WIDTH ** -0.5),
        "lru_conv_b": nrm(ks[9], (N_EVEN, LRU_WIDTH), 0.02),
        "lru_wa": nrm(ks[10], (N_EVEN, LRU_BLOCKS, LRU_BLOCK_DIM, LRU_BLOCK_DIM), LRU_BLOCK_DIM ** -0.5),
        "lru_ba": nrm(ks[11], (N_EVEN, LRU_BLOCKS, LRU_BLOCK_DIM), 0.02),
        "lru_wx": nrm(ks[12], (N_EVEN, LRU_BLOCKS, LRU_BLOCK_DIM, LRU_BLOCK_DIM), LRU_BLOCK_DIM ** -0.5),
        "lru_bx": nrm(ks[13], (N_EVEN, LRU_BLOCKS, LRU_BLOCK_DIM), 0.02),
        "lru_lambda": jnp.log(sig) - jnp.log1p(-sig),
        "mix0_out_w": nrm(ks[15], (N_EVEN, MIX0_WIDTH, D), MIX0_WIDTH ** -0.5),
        "mix1_in_w": nrm(ks[16], (N_ODD, D, IN1_WIDTH), D ** -0.5),
        "cmp_pos": nrm(ks[17], (N_ODD, 2, CMP_LEN, HEAD_DIM), 0.1),
        "cmp_w1": nrm(ks[18], (N_ODD, 2, CMP_LEN * HEAD_DIM, CMP_HIDDEN), (CMP_LEN * HEAD_DIM) ** -0.5),
        "cmp_w2": nrm(ks[19], (N_ODD, 2, CMP_HIDDEN, HEAD_DIM), CMP_HIDDEN ** -0.5),
        "mix1_out_w": nrm(ks[20], (N_ODD, MIX1_WIDTH, D), MIX1_WIDTH ** -0.5),
        "final_norm_g": 1.0 + nrm(ks[21], (D,), 0.05),
    }


def reference(x, c, mod_w, mod_b, norm_g, ffn_w1, ffn_w2, mix0_in_w, lru_conv_w, lru_conv_b,
              lru_wa, lru_ba, lru_wx, lru_bx, lru_lambda, mix0_out_w, mix1_in_w, cmp_pos,
              cmp_w1, cmp_w2, mix1_out_w, final_norm_g):
    B, S, D = x.shape
    pos = jnp.arange(S, dtype=jnp.float32)
    cond = jax.nn.silu(c)
    for l in range(DEPTH):
        mod = (cond @ mod_w[l] + mod_b[l]).reshape(B, 9, D)
        h = modulate(rmsnorm(x, norm_g[l, 0]), mod[:, 0], mod[:, 1])
        x = x + 0.5 * mod[:, 2][:, None, :] * swiglu(h, ffn_w1[l, 0], ffn_w2[l, 0])
        h = modulate(rmsnorm(x, norm_g[l, 1]), mod[:, 3], mod[:, 4])
        j = l // 2
        if l % 2 == 0:
            y = mixer_lru_moba(h, mix0_in_w[j], lru_conv_w[j], lru_conv_b[j], lru_wa[j], lru_ba[j],
                               lru_wx[j], lru_bx[j], lru_lambda[j], mix0_out_w[j], pos)
        else:
            y = mixer_nsa(h, mix1_in_w[j], cmp_pos[j], cmp_w1[j], cmp_w2[j], mix1_out_w[j], pos)
        x = x + mod[:, 5][:, None, :] * y
        h = modulate(rmsnorm(x, norm_g[l, 2]), mod[:, 6], mod[:, 7])
        x = x + 0.5 * mod[:, 8][:, None, :] * swiglu(h, ffn_w1[l, 1], ffn_w2[l, 1])
    return rmsnorm(x, final_norm_g)
```

```python
import numpy as np
import ml_dtypes
from contextlib import ExitStack
import concourse.bass as bass
import concourse.mybir as mybir
from concourse.bass_utils import run_bass_kernel_spmd

F32 = mybir.dt.float32
BF16 = mybir.dt.bfloat16
AF = mybir.ActivationFunctionType
ALU = mybir.AluOpType
AX = mybir.AxisListType


class T:
    def __init__(self, h, name):
        self.h = h
        self.name = name
        self.writers = {}
        self.readers = {}
        self.is_dram = False

    def __getitem__(self, k):
        return self.h[k]


class Op:
    __slots__ = ("stream", "agent", "fn", "deps", "isdma", "iscc")


class Prog:
    STREAMS = ("pe", "act", "dve", "pool", "sp")

    def __init__(self):
        self.nc = bass.Bass("TRN2", target_bir_lowering=False)
        self.es = ExitStack()
        self.tes = ExitStack()
        self.ops = []
        self.ntile = 0
        self.drams = []
        self.cnt = {}
        self.sems = {}
        self.free_dma_sems = []
        self.tot = dict(nops=0, nwait=0)

    def dram_in(self, name, shape, dt):
        return self._mkdram(self.nc.dram_tensor(name, list(shape), dt, kind="ExternalInput").ap(), name)

    def dram_out(self, name, shape, dt):
        return self._mkdram(self.nc.dram_tensor(name, list(shape), dt, kind="ExternalOutput").ap(), name)

    def dram_tmp(self, name, shape, dt):
        return self._mkdram(self.nc.dram_tensor(name, list(shape), dt, kind="Internal").ap(), name)

    def _mkdram(self, h, name):
        t = T(h, name)
        t.is_dram = True
        self.drams.append(t)
        return t

    def sb(self, name, shape, dt):
        self.ntile += 1
        h = self.tes.enter_context(self.nc.sbuf_tensor(f"{name}_{self.ntile}", list(shape), dt))
        return T(h, name)

    def ps(self, name, shape, dt=F32):
        self.ntile += 1
        h = self.tes.enter_context(self.nc.psum_tensor(f"{name}_{self.ntile}", list(shape), dt))
        return T(h, name)

    def op(self, stream, fn, r=(), w=(), wp=(), dma=False, dma_tile=None, cc=False):
        o = Op()
        o.stream = stream
        o.isdma = dma
        o.iscc = cc
        o.agent = ("q_%d" % id(dma_tile)) if dma else stream
        o.fn = fn
        idx = len(self.ops)
        deps = set()
        inorder = not dma
        for t in r:
            for a, j in t.writers.items():
                deps.add(j)
        for t in w:
            for a, j in t.writers.items():
                if not (inorder and a == o.agent):
                    deps.add(j)
            for a, j in t.readers.items():
                if not (inorder and a == o.agent):
                    deps.add(j)
        for t in wp:
            for a, j in t.readers.items():
                if not (inorder and a == o.agent):
                    deps.add(j)
        if stream == "pe" and not dma:
            deps = {j for j in deps if not (self.ops[j].agent == "pe")}
        o.deps = deps
        self.ops.append(o)
        for t in r:
            t.readers[o.agent] = idx
        for t in w:
            if t.readers:
                t.writers = {}
                t.readers = {}
            t.writers[o.agent] = idx
        for t in wp:
            if t.readers:
                t.writers = {}
                t.readers = {}
            t.writers[o.agent] = idx
        return idx

    def dma(self, out_t, out_ap, in_t, in_ap, q="sp", part=False, **kw):
        eng = self._eng(q)
        w = () if part else (out_t,)
        wp = (out_t,) if part else ()
        sbt = in_t if out_t.is_dram else out_t
        assert not sbt.is_dram
        self._keep = getattr(self, "_keep", [])
        self._keep.append(sbt)
        return self.op(q, lambda: eng.dma_start(out=out_ap, in_=in_ap, **kw), r=(in_t,), w=w, wp=wp, dma=True, dma_tile=sbt)

    def _eng(self, s):
        nc = self.nc
        return {"pe": nc.tensor, "act": nc.scalar, "dve": nc.vector, "pool": nc.gpsimd, "sp": nc.sync}[s]

    def allgather(self, in_t, out_t, groups, in_ap=None):
        nc = self.nc
        self._cck = getattr(self, "_cck", 0) + 1
        key = T(None, f"cc{self._cck}")
        self._keep = getattr(self, "_keep", [])
        self._keep.append(key)
        return self.op("pool", lambda: nc.gpsimd.collective_compute("AllGather", ALU.bypass, replica_groups=groups,
                                                                    ins=[in_t[:, :] if in_ap is None else in_ap], outs=[out_t[:, :]]),
                       r=(in_t,), w=(out_t,), dma=True, dma_tile=key, cc=True)

    def emit(self, final=True):
        nc = self.nc
        ops = self.ops
        sig = [False] * len(ops)
        for o in ops:
            for d in o.deps:
                sig[d] = True
        for i, o in enumerate(ops):
            if o.isdma:
                sig[i] = True
        cnt = self.cnt
        sems = self.sems
        val = [0] * len(ops)
        phase_agents = []
        for i, o in enumerate(ops):
            if sig[i]:
                if o.agent not in sems:
                    if o.isdma and not o.iscc and self.free_dma_sems:
                        s_, c_ = self.free_dma_sems.pop()
                        sems[o.agent] = s_
                        cnt[o.agent] = c_
                    else:
                        sems[o.agent] = self.es.enter_context(nc.semaphore(f"sem_{len(sems)}_{self.ntile}"))
                        cnt[o.agent] = 0
                if o.agent not in phase_agents:
                    phase_agents.append(o.agent)
                cnt[o.agent] += (1 if (o.iscc or not o.isdma) else 16)
                val[i] = cnt[o.agent]
        seen = {s: {} for s in self.STREAMS}
        nwait = 0
        for i, o in enumerate(ops):
            eng = self._eng(o.stream)
            need = {}
            for d in o.deps:
                a = ops[d].agent
                need[a] = max(need.get(a, 0), val[d])
            for a, v in need.items():
                if seen[o.stream].get(a, 0) < v:
                    eng.wait_ge(sems[a], v)
                    seen[o.stream][a] = v
                    nwait += 1
            ins = o.fn()
            if sig[i]:
                if o.iscc:
                    ins.then_inc(sems[o.agent])
                else:
                    ins.then_inc(sems[o.agent], 16 if o.isdma else 1)
        for s in self.STREAMS:
            eng = self._eng(s)
            for a in phase_agents:
                if seen[s].get(a, 0) < cnt[a]:
                    eng.wait_ge(sems[a], cnt[a])
        self.tot["nops"] += len(ops)
        self.tot["nwait"] += nwait
        self.stats = dict(self.tot)
        for a in phase_agents:
            if a.startswith("q_") and not any(o.iscc and o.agent == a for o in ops):
                self.free_dma_sems.append((sems.pop(a), cnt.pop(a)))
        self.ops = []
        for t in self.drams:
            t.writers = {}
            t.readers = {}
        self.tes.close()
        self.tes = ExitStack()
        return nc


D = 1024
DFF = 2816
NJ = DFF // 128
NT = 4096
TG = 256
EPS = 1e-6


def tp_phase(P, cfg, io):
    nc = P.nc
    pre, post, final = cfg.get("pre"), cfg.get("post"), cfg.get("final", False)
    ffn_norm, ffn_mod = cfg["ffn"]
    x_in, c_in, modw, modb, ng, w1, w2, identd, x_out = (io[k] for k in ("x_in", "c", "modw", "modb", "ng", "w1", "w2", "ident", "x_out"))
    if pre:
        nfm, tmw = pre["nfm"], pre["tmw"]
        wout = io["wout"]
        xs_d = io["xs"]
    if post:
        wc = post["wc"]
        win = io["win"]
    if final:
        fgd = io["fg"]

    wbig = P.sb("wbig", [128, 8 * 2 * DFF], BF16)
    w2b = P.sb("w2b", [128, NJ, D], BF16)
    stage = [P.sb(f"stage{i}", [128, 1024], F32) for i in range(2)]
    ident_f = P.sb("identf", [128, 128], F32)
    ident = P.sb("ident", [128, 128], BF16)
    ones_f = P.sb("ones", [128, 128], F32)
    cT = P.sb("cT", [128, 8], F32)
    cbc = P.sb("cbc", [128, 8, 128], F32)
    rowA = P.sb("rowA", [128, D], F32)
    rowB = P.sb("rowB", [128, D], F32)
    rowsm = P.sb("rowsm", [1, D], F32)
    xbuf = [P.sb(f"xb{i}", [128, D], F32) for i in range(2)]
    xrb = [P.sb(f"xr{i}", [128, D], F32) for i in range(2)]
    hb = [P.sb(f"hb{i}", [128, D], BF16) for i in range(2)]
    junk = P.sb("junk", [128, D], BF16)
    sm = [P.sb(f"sm{i}", [128, 4], F32) for i in range(2)]
    mhalf = P.sb("mhalf", [128, 1], F32)
    hT = [P.sb(f"hT{i}", [128, 8, TG], BF16) for i in range(2)]
    actT = P.sb("actT", [128, NJ, TG], BF16)
    sa = [P.sb(f"sa{i}", [128, TG], F32) for i in range(2)]
    if final:
        rowF = P.sb("rowF", [128, D], F32)
    pa = [P.ps(f"pa{i}", [128, 512]) for i in range(2)]
    pb = [P.ps(f"pb{i}", [128, 512]) for i in range(2)]
    po = [P.ps(f"po{i}", [128, 512]) for i in range(2)]
    ptr = [P.ps(f"ptr{i}", [128, 1024], BF16) for i in range(2)]

    cnt = {"bl": 0, "st": 0, "x": 0, "po": 0, "pa": 0, "ptr": 0, "hb": 0, "sa": 0, "xr": 0, "xo": 0, "sm": 0}

    def nxt(k, n=2):
        v = cnt[k] % n
        cnt[k] += 1
        return v

    P.dma(ident_f, ident_f[:, :], identd, identd[:, :])
    P.op("pool", lambda: nc.gpsimd.tensor_copy(out=ident[:, :], in_=ident_f[:, :]), r=[ident_f], w=[ident])
    P.op("pool", lambda: nc.gpsimd.memset(ones_f[:, :], 1.0), w=[ones_f])
    P.op("pool", lambda: nc.gpsimd.memset(mhalf[:, :], -0.5), w=[mhalf])
    P.dma(cT, cT[:, :], c_in, c_in[:, :])
    P.op("act", lambda: nc.scalar.activation(out=cT[:, :], in_=cT[:, :], func=AF.Silu), r=[cT], w=[cT])
    for kc in range(8):
        P.op("dve", lambda kc=kc: nc.vector.tensor_scalar(out=cbc[:, kc, :], in0=ones_f[:, :], scalar1=cT[:, kc:kc + 1],
                                                         scalar2=None, op0=ALU.mult), r=[ones_f, cT], wp=[cbc])

    def mod_piece(idx, dst, plus1=False):
        P.dma(rowsm, rowsm[0:1, :], modb, modb[0:1, idx * D:(idx + 1) * D])
        pp = [po[0], po[1]]
        for kc in range(8):
            st = stage[nxt("st")]
            P.dma(st, st[:, 0:D], modw, modw[kc * 128:(kc + 1) * 128, idx * D:(idx + 1) * D])
            for h in range(2):
                P.op("pe", lambda kc=kc, h=h, st=st: nc.tensor.matmul(pp[h][:, :], lhsT=cbc[:, kc, :], rhs=st[:, h * 512:(h + 1) * 512],
                                                                     start=(kc == 0), stop=False),
                     r=[cbc, st], w=[pp[h]] if kc == 0 else (), wp=[pp[h]] if kc else ())
        for h in range(2):
            P.op("pe", lambda h=h: nc.tensor.matmul(pp[h][:, :], lhsT=ones_f[0:1, :], rhs=rowsm[0:1, h * 512:(h + 1) * 512],
                                                    start=False, stop=True), r=[ones_f, rowsm], wp=[pp[h]])
            if plus1:
                P.op("dve", lambda h=h: nc.vector.tensor_scalar_add(out=dst[:, h * 512:(h + 1) * 512], in0=pp[h][:, :], scalar1=1.0),
                     r=[pp[h]], wp=[dst])
            else:
                P.op("dve", lambda h=h: nc.vector.tensor_copy(out=dst[:, h * 512:(h + 1) * 512], in_=pp[h][:, :]), r=[pp[h]], wp=[dst])

    def row_bc(src_t, src_ap, dst, mul_into=False):
        rowsm2 = rowsm
        P.dma(rowsm2, rowsm2[0:1, :], src_t, src_ap)
        for h in range(2):
            pp = po[h]
            P.op("pe", lambda h=h, pp=pp: nc.tensor.matmul(pp[:, :], lhsT=ones_f[0:1, :], rhs=rowsm2[0:1, h * 512:(h + 1) * 512],
                                                           start=True, stop=True), r=[ones_f, rowsm2], w=[pp])
            if mul_into:
                P.op("dve", lambda h=h, pp=pp: nc.vector.tensor_tensor(out=dst[:, h * 512:(h + 1) * 512], in0=dst[:, h * 512:(h + 1) * 512],
                                                                       in1=pp[:, :], op=ALU.mult), r=[pp, dst], wp=[dst])
            else:
                P.op("dve", lambda h=h, pp=pp: nc.vector.tensor_copy(out=dst[:, h * 512:(h + 1) * 512], in_=pp[:, :]), r=[pp], wp=[dst])

    cast_engs = ["pool", "dve", "act"]

    def cast(eng, out_ap, in_ap, r, wp, mul_ap=None, mul_t=None):
        if mul_ap is not None:
            e = "pool" if eng == "act" else eng
            ee = nc.gpsimd if e == "pool" else nc.vector
            P.op(e, lambda: ee.tensor_tensor(out=out_ap, in0=in_ap, in1=mul_ap, op=ALU.mult), r=list(r) + [mul_t], wp=wp)
        elif eng == "act":
            P.op("act", lambda: nc.scalar.copy(out=out_ap, in_=in_ap), r=r, wp=wp)
        else:
            ee = nc.gpsimd if eng == "pool" else nc.vector
            P.op(eng, lambda: ee.tensor_copy(out=out_ap, in_=in_ap), r=r, wp=wp)

    def load_w(dst, dst_view, src_t, rows_kc, ncols, mul_t=None):
        k = 0
        for kc in range(rows_kc):
            for c0 in range(0, ncols, 1024):
                cw = min(1024, ncols - c0)
                st = stage[nxt("st")]
                P.dma(st, st[:, 0:cw], src_t, src_t[kc * 128:(kc + 1) * 128, c0:c0 + cw])
                cast(cast_engs[k % 3], dst_view(kc)[:, c0:c0 + cw], st[:, 0:cw], [st], [dst],
                     mul_ap=(mul_t[:, c0:c0 + cw] if mul_t is not None else None), mul_t=mul_t)
                k += 1

    W1C = 2 * DFF

    def w1v(kc):
        return wbig[:, kc * W1C:(kc + 1) * W1C]

    mod_piece(ffn_mod + 1, rowA, plus1=True)
    row_bc(ng, ng[ffn_norm:ffn_norm + 1, :], rowA, mul_into=True)
    mod_piece(ffn_mod + 0, rowB)
    rtmp = xbuf[0]
    mod_piece(ffn_mod + 2, rtmp)
    P.op("dve", lambda: nc.vector.tensor_scalar_mul(out=rtmp[:, :], in0=rtmp[:, :], scalar1=0.5), r=[rtmp], w=[rtmp])
    load_w(w2b, lambda j: w2b[:, j, :], w2, NJ, D, mul_t=rtmp)
    xsrc = x_in
    if pre:
        xsrc = xs_d
        for (gi_t, gi_ap, go_t) in io.get("gathers", []):
            P.allgather(gi_t, go_t, PAIRS, in_ap=gi_ap)
        selt = P.sb("selt", [128, 2], F32)
        P.dma(selt, selt[:, :], io["sel"], io["sel"][:, :])

        def blend_load(ncols, parts):
            sa_, sb_ = ((stage[0], stage[1]), (xbuf[0], xbuf[1]))[nxt("bl")]
            for (c0, wd, A_t, A_ap, B_t, B_ap, vw) in parts:
                P.dma(sa_, vw(sa_, c0, wd), A_t, A_ap, part=True)
                P.dma(sb_, vw(sb_, c0, wd), B_t, B_ap, part=True)
            P.op("act", lambda: nc.scalar.activation(out=sa_[:, 0:ncols], in_=sa_[:, 0:ncols], func=AF.Copy, scale=selt[:, 0:1]), r=[sa_, selt], w=[sa_])
            P.op("dve", lambda: nc.vector.scalar_tensor_tensor(out=sa_[:, 0:ncols], in0=sb_[:, 0:ncols], scalar=selt[:, 1:2], in1=sa_[:, 0:ncols],
                                                               op0=ALU.mult, op1=ALU.add), r=[sa_, sb_, selt], w=[sa_])
            return sa_

        mod_piece(5, rtmp)
        load_w(wbig, lambda kc: wbig[:, kc * D:(kc + 1) * D], wout, 8, D, mul_t=rtmp)
        ntm = tmw // 128
        for t in range(NT // 128):
            tok0 = t * 128
            yTt = hT[t % 2]
            xt = xrb[nxt("xr")]
            P.dma(xt, xt[:, :], x_in, x_in[tok0:tok0 + 128, :])
            if nfm:
                A_t, A_ap, B_t, B_ap = io["fm_cand"](tok0)
                st = blend_load(nfm * 128, [(0, nfm * 128, A_t, A_ap.rearrange("(k p) c -> p k c", p=128), B_t, B_ap.rearrange("(k p) c -> p k c", p=128),
                                            lambda tl, c0, wd: tl[:, c0:c0 + wd].rearrange("p (k c) -> p k c", k=nfm))])
                P.op("pool", lambda st=st, yTt=yTt: nc.gpsimd.tensor_copy(out=yTt[:, 0:nfm, 0:128],
                                                                      in_=st[:, 0:nfm * 128].rearrange("p (k c) -> p k c", k=nfm)),
                     r=[st], wp=[yTt])
            st2 = blend_load(tmw, [(c0, wd, A_t, A_ap, B_t, B_ap, lambda tl, c0, wd: tl[:, c0:c0 + wd])
                                   for (c0, wd, A_t, A_ap, B_t, B_ap) in io["tm_cand"](tok0)])
            h = hb[nxt("hb")]
            P.op("pool", lambda st2=st2, h=h: nc.gpsimd.tensor_copy(out=h[:, 0:tmw], in_=st2[:, 0:tmw]), r=[st2], w=[h])
            pt = ptr[nxt("ptr")]
            for kc in range(ntm):
                P.op("pe", lambda kc=kc, pt=pt, h=h: nc.tensor.transpose(pt[:, kc * 128:(kc + 1) * 128], h[:, kc * 128:(kc + 1) * 128], ident[:, :]),
                     r=[h, ident], w=[pt] if kc == 0 else (), wp=[pt] if kc else ())
            P.op("act", lambda pt=pt, yTt=yTt: nc.scalar.copy(out=yTt[:, nfm:nfm + ntm, 0:128],
                                                          in_=pt[:, 0:ntm * 128].rearrange("p (k c) -> p k c", k=ntm)),
                 r=[pt], wp=[yTt])
            for h2 in range(2):
                pp = po[nxt("po")]
                for kc in range(8):
                    P.op("pe", lambda kc=kc, pp=pp, h2=h2, yTt=yTt: nc.tensor.matmul(pp[:, :], lhsT=yTt[:, kc, 0:128],
                                                                                   rhs=wbig[:, kc * D + h2 * 512:kc * D + (h2 + 1) * 512],
                                                                                   start=(kc == 0), stop=(kc == 7)),
                         r=[yTt, wbig], w=[pp] if kc == 0 else (), wp=[pp] if kc else ())
                P.op("dve", lambda pp=pp, xt=xt, h2=h2: nc.vector.tensor_tensor(out=xt[:, h2 * 512:(h2 + 1) * 512], in0=pp[:, :],
                                                                              in1=xt[:, h2 * 512:(h2 + 1) * 512], op=ALU.add),
                     r=[pp, xt], w=[xt])
            P.dma(xs_d, xs_d[tok0:tok0 + 128, :], xt, xt[:, :], q="pool", part=True)
    load_w(wbig, w1v, w1, 8, W1C)
    if final:
        row_bc(fgd, fgd[0:1, :], rowF)

    def rms_rstd(xt_t, xt_ap):
        s = sm[nxt("sm")]
        P.op("dve", lambda: nc.vector.scalar_tensor_tensor(out=junk[:, :], in0=xt_ap, scalar=1.0, in1=xt_ap, op0=ALU.mult, op1=ALU.mult,
                                                           accum_out=s[:, 0:1]), r=[xt_t], w=[junk, s])
        P.op("pool", lambda: nc.gpsimd.tensor_scalar(out=s[:, 1:2], in0=s[:, 0:1], scalar1=1.0 / D, scalar2=EPS, op0=ALU.mult, op1=ALU.add),
             r=[s], w=[s])
        P.op("pool", lambda: nc.gpsimd.tensor_tensor(out=s[:, 2:3], in0=s[:, 1:2], in1=mhalf[:, 0:1], op=ALU.pow), r=[s, mhalf], w=[s])
        return s

    def norm_mod(xt_t, A, B):
        s = rms_rstd(xt_t, xt_t[:, :])
        h = hb[nxt("hb")]
        P.op("dve", lambda: nc.vector.scalar_tensor_tensor(out=xt_t[:, :], in0=xt_t[:, :], scalar=s[:, 2:3], in1=A[:, :], op0=ALU.mult, op1=ALU.mult),
             r=[xt_t, s, A], w=[xt_t])
        P.op("dve", lambda: nc.vector.tensor_tensor(out=h[:, :], in0=xt_t[:, :], in1=B[:, :], op=ALU.add), r=[xt_t, B], w=[h])
        return h

    def transpose_into(h, hTt, col0, nk=8):
        pt = ptr[nxt("ptr")]
        for kc in range(nk):
            P.op("pe", lambda kc=kc: nc.tensor.transpose(pt[:, kc * 128:(kc + 1) * 128], h[:, kc * 128:(kc + 1) * 128], ident[:, :]),
                 r=[h, ident], w=[pt] if kc == 0 else (), wp=[pt] if kc else ())
        P.op("act", lambda: nc.scalar.copy(out=hTt[:, 0:nk, col0:col0 + 128], in_=pt[:, 0:nk * 128].rearrange("p (k c) -> p k c", k=nk)),
             r=[pt], wp=[hTt])

    NG = NT // TG
    TPG = TG // 128

    hpend = {}

    def prep(g):
        hTt = hT[g % 2]
        hs = []
        for i in range(TPG):
            tok0 = g * TG + i * 128
            xt = xbuf[nxt("x")]
            P.dma(xt, xt[:, :], xsrc, xsrc[tok0:tok0 + 128, :])
            hs.append(norm_mod(xt, rowA, rowB))
        hpend[g] = hs

    def trans(g):
        for i, h in enumerate(hpend.pop(g)):
            transpose_into(h, hT[g % 2], i * 128)

    prep(0)
    trans(0)
    for g in range(NG):
        hTt = hT[g % 2]
        for j in range(NJ):
            if j == 8 and g + 1 < NG:
                prep(g + 1)
            k = nxt("pa")
            ppa, ppb = pa[k], pb[k]
            for (pp, col) in ((ppa, j * 128), (ppb, DFF + j * 128)):
                for kc in range(8):
                    P.op("pe", lambda kc=kc, pp=pp, col=col, hTt=hTt: nc.tensor.matmul(pp[:, 0:TG], lhsT=wbig[:, kc * W1C + col:kc * W1C + col + 128],
                                                                                     rhs=hTt[:, kc, :], start=(kc == 0), stop=(kc == 7)),
                         r=[wbig, hTt], w=[pp] if kc == 0 else (), wp=[pp] if kc else ())
            s_ = sa[nxt("sa")]
            P.op("act", lambda s_=s_, ppa=ppa: nc.scalar.activation(out=s_[:, :], in_=ppa[:, 0:TG], func=AF.Silu), r=[ppa], w=[s_])
            P.op("dve", lambda s_=s_, ppb=ppb, j=j: nc.vector.tensor_tensor(out=actT[:, j, :], in0=s_[:, :], in1=ppb[:, 0:TG], op=ALU.mult),
                 r=[s_, ppb], wp=[actT])
        if g + 1 < NG:
            trans(g + 1)
        for i in range(TPG):
            tok0 = g * TG + i * 128
            xr = xrb[nxt("xr")]
            P.dma(xr, xr[:, :], xsrc, xsrc[tok0:tok0 + 128, :])
            xo = xr
            for h2 in range(2):
                pp = po[nxt("po")]
                for j in range(NJ):
                    P.op("pe", lambda j=j, pp=pp, h2=h2, i=i: nc.tensor.matmul(pp[:, :], lhsT=actT[:, j, i * 128:(i + 1) * 128],
                                                                             rhs=w2b[:, j, h2 * 512:(h2 + 1) * 512], start=(j == 0), stop=(j == NJ - 1)),
                         r=[actT, w2b], w=[pp] if j == 0 else (), wp=[pp] if j else ())
                P.op("dve", lambda pp=pp, xr=xr, xo=xo, h2=h2: nc.vector.tensor_tensor(out=xo[:, h2 * 512:(h2 + 1) * 512], in0=pp[:, :],
                                                                                     in1=xr[:, h2 * 512:(h2 + 1) * 512], op=ALU.add),
                     r=[pp, xr], w=[xo])
            if final:
                s = rms_rstd(xo, xo[:, :])
                P.op("dve", lambda xo=xo, s=s: nc.vector.scalar_tensor_tensor(out=xo[:, :], in0=xo[:, :], scalar=s[:, 2:3], in1=rowF[:, :],
                                                                            op0=ALU.mult, op1=ALU.mult), r=[xo, s, rowF], w=[xo])
            P.dma(x_out, x_out[tok0:tok0 + 128, :], xo, xo[:, :], q="pool", part=True)

    if post:
        pn, pm = post["norm_idx"], post["mod_idx0"]
        mod_piece(pm + 1, rowA, plus1=True)
        row_bc(ng, ng[pn:pn + 1, :], rowA, mul_into=True)
        mod_piece(pm + 0, rowB)
        load_w(wbig, lambda kc: wbig[:, kc * wc:(kc + 1) * wc], win, 8, wc)
        uo = [xrb[0], xrb[1], sa[0], sa[1]]
        cnt["uo"] = 0
        ppend = {}

        def prepP(g):
            hs = []
            for i in range(TPG):
                tok0 = g * TG + i * 128
                xt = xbuf[nxt("x")]
                P.dma(xt, xt[:, :], x_out, x_out[tok0:tok0 + 128, :])
                hs.append(norm_mod(xt, rowA, rowB))
            ppend[g] = hs

        def transP(g):
            for i, h in enumerate(ppend.pop(g)):
                transpose_into(h, hT[g % 2], i * 128)

        prepP(0)
        transP(0)
        for g in range(NG):
            hTt = hT[g % 2]
            if g + 1 < NG:
                prepP(g + 1)
            for m, (col, dstT, row0) in enumerate(post["fm"]):
                if m == len(post["fm"]) // 2 and g + 1 < NG:
                    transP(g + 1)
                pp = pa[nxt("pa")]
                for kc in range(8):
                    P.op("pe", lambda kc=kc, pp=pp, col=col, hTt=hTt: nc.tensor.matmul(pp[:, 0:TG], lhsT=wbig[:, kc * wc + col:kc * wc + col + 128],
                                                                                     rhs=hTt[:, kc, :], start=(kc == 0), stop=(kc == 7)),
                         r=[wbig, hTt], w=[pp] if kc == 0 else (), wp=[pp] if kc else ())
                u = uo[nxt("uo", 4)]
                if m % 2 == 0:
                    P.op("act", lambda u=u, pp=pp: nc.scalar.copy(out=u[:, 0:TG], in_=pp[:, 0:TG]), r=[pp], w=[u])
                else:
                    P.op("dve", lambda u=u, pp=pp: nc.vector.tensor_copy(out=u[:, 0:TG], in_=pp[:, 0:TG]), r=[pp], w=[u])
                P.dma(dstT, dstT[row0:row0 + 128, g * TG:(g + 1) * TG], u, u[:, 0:TG], q="pool", part=True)
            for i in range(TPG):
                tok0 = g * TG + i * 128
                for (col, wd, dstT, oc) in post["tm"]:
                    pp = po[nxt("po")]
                    for kc in range(8):
                        P.op("pe", lambda kc=kc, pp=pp, col=col, wd=wd, hTt=hTt, i=i: nc.tensor.matmul(pp[:, 0:wd], lhsT=hTt[:, kc, i * 128:(i + 1) * 128],
                                                                                                   rhs=wbig[:, kc * wc + col:kc * wc + col + wd],
                                                                                                   start=(kc == 0), stop=(kc == 7)),
                             r=[wbig, hTt], w=[pp] if kc == 0 else (), wp=[pp] if kc else ())
                    u = uo[nxt("uo", 4)]
                    P.op("dve", lambda u=u, pp=pp, wd=wd: nc.vector.tensor_copy(out=u[:, 0:wd], in_=pp[:, 0:wd]), r=[pp], w=[u])
                    P.dma(dstT, dstT[tok0:tok0 + 128, oc:oc + wd], u, u[:, 0:wd], q="pool", part=True)
    P.emit()


S = 8192
TC = 2048
NEGB = -30000.0


def mx0_phase(P, io):
    nc = P.nc

    def V(eng, name, r, w=(), wp=(), **kw):
        e = P._eng(eng)
        P.op(eng, lambda: getattr(e, name)(**kw), r=r, w=w, wp=wp)

    uF, vd, lrup, wad, wxd, ropeC, ropeS, onehot, cmd, pastd, ownd, identd, yl, ya = (io[k] for k in (
        "uF", "vF", "lrup", "wa", "wx", "ropeC", "ropeS", "onehot", "cm", "past", "own", "ident", "ylF", "yaF"))

    ident_f = P.sb("identf", [128, 128], F32)
    cm = P.sb("cm", [128, 4 * 512], BF16)
    past = P.sb("past", [128, 32 * 32], F32)
    own = P.sb("own", [128, 32 * 32], F32)
    P.dma(ident_f, ident_f[:, :], identd, identd[:, :])
    P.dma(cm, cm[:, :], cmd, cmd[:, :])
    P.dma(past, past[:, :], pastd, pastd[:, :])
    P.dma(own, own[:, :], ownd, ownd[:, :])

    lp = P.sb("lp", [128, 2, 8], F32)
    wa = P.sb("wa", [128, 2, 128], F32)
    wx = P.sb("wx", [128, 2, 128], F32)
    c1 = P.sb("c1", [128, 2], F32)
    P.dma(lp, lp[:, :, :], lrup, lrup[:, :].rearrange("(c p) k -> p c k", p=128))
    P.dma(wa, wa[:, :, :], wad, wad[:, :].rearrange("(c p) k -> p c k", p=128))
    P.dma(wx, wx[:, :, :], wxd, wxd[:, :].rearrange("(c p) k -> p c k", p=128))
    V("act", "activation", [lp], w=[c1], out=c1[:, :], in_=lp[:, :, 7], func=AF.Exp, scale=-1.0)
    V("act", "activation", [c1], w=[c1], out=c1[:, :], in_=c1[:, :], func=AF.Ln, bias=1.0, scale=1.0)
    V("dve", "tensor_scalar_mul", [c1], w=[c1], out=c1[:, :], in0=c1[:, :], scalar1=-8.0)

    xp = [P.sb(f"xp{i}", [128, TC + 3], F32) for i in range(2)]
    xc = P.sb("xc", [128, TC], F32)
    rr = P.sb("rr", [128, TC], F32)
    ii = P.sb("ii", [128, TC], F32)
    aa = P.sb("aa", [128, TC], F32)
    t1 = P.sb("t1", [128, TC], F32)
    hh = [P.sb(f"hh{i}", [128, TC], F32) for i in range(2)]
    gg = P.sb("gg", [128, TC], F32)
    g2 = P.sb("g2", [128, TC], F32)
    pl = [P.ps(f"pl{i}", [128, 512]) for i in range(2)]
    nps = 0

    def lru_step(c, tcn):
        nonlocal nps
        if True:
            k = (c * (S // TC) + tcn)
            xpt = xp[k % 2]
            xpn = xp[(k + 1) % 2]
            t0 = tcn * TC
            if tcn == 0:
                V("pool", "memset", [], wp=[xpt], ap=xpt[:, 0:3], constant=0.0)
            P.dma(xpt, xpt[:, 3:3 + TC], uF, uF[c * 128:(c + 1) * 128, t0:t0 + TC], part=True)
            P.dma(gg, gg[:, :], uF, uF[256 + c * 128:256 + (c + 1) * 128, t0:t0 + TC])
            V("dve", "tensor_scalar", [xpt, lp], w=[xc], out=xc[:, :], in0=xpt[:, 3:3 + TC], scalar1=lp[:, c, 3:4], scalar2=lp[:, c, 4:5],
              op0=ALU.mult, op1=ALU.add)
            for kk in range(3):
                V("dve", "scalar_tensor_tensor", [xpt, lp, xc], w=[xc], out=xc[:, :], in0=xpt[:, kk:kk + TC], scalar=lp[:, c, kk:kk + 1], in1=xc[:, :],
                  op0=ALU.mult, op1=ALU.add)
            if tcn + 1 < S // TC:
                V("pool", "tensor_copy", [xpt], wp=[xpn], out=xpn[:, 0:3], in_=xpt[:, TC:TC + 3])
            for n in range(TC // 512):
                for (wm, dst, bcol) in ((wa, rr, 5), (wx, ii, 6)):
                    pp = pl[nps % 2]
                    nps += 1
                    P.op("pe", lambda pp=pp, wm=wm, n=n, c=c: nc.tensor.matmul(pp[:, :], lhsT=wm[:, c, :], rhs=xc[:, n * 512:(n + 1) * 512], start=True, stop=True),
                         r=[wm, xc], w=[pp])
                    V("act", "activation", [pp, lp], wp=[dst], out=dst[:, n * 512:(n + 1) * 512], in_=pp[:, :], func=AF.Sigmoid, bias=lp[:, c, bcol:bcol + 1], scale=1.0)
            V("act", "activation", [rr, c1], w=[aa], out=aa[:, :], in_=rr[:, :], func=AF.Exp, scale=c1[:, c:c + 1])
            V("pool", "tensor_tensor", [aa], w=[t1], out=t1[:, :], in0=aa[:, :], in1=aa[:, :], op=ALU.mult)
            V("act", "activation", [t1], w=[t1], out=t1[:, :], in_=t1[:, :], func=AF.Sqrt, bias=1.0, scale=-1.0)
            V("pool", "tensor_tensor", [ii, xc], w=[ii], out=ii[:, :], in0=ii[:, :], in1=xc[:, :], op=ALU.mult)
            V("pool", "tensor_tensor", [ii, t1], w=[ii], out=ii[:, :], in0=ii[:, :], in1=t1[:, :], op=ALU.mult)
            ht = hh[k % 2]
            hprev = hh[(k + 1) % 2]
            if tcn == 0:
                V("dve", "tensor_tensor_scan", [aa, ii], w=[ht], out=ht[:, :], data0=aa[:, :], data1=ii[:, :], initial=0.0, op0=ALU.mult, op1=ALU.add)
            else:
                V("dve", "tensor_tensor_scan", [aa, ii, hprev], w=[ht], out=ht[:, :], data0=aa[:, :], data1=ii[:, :], initial=hprev[:, TC - 1:TC],
                  op0=ALU.mult, op1=ALU.add)
            V("pool", "tensor_tensor", [gg], w=[g2], out=g2[:, :], in0=gg[:, :], in1=gg[:, :], op=ALU.mult)
            V("pool", "tensor_scalar", [g2], w=[g2], out=g2[:, :], in0=g2[:, :], scalar1=0.044715, scalar2=1.0, op0=ALU.mult, op1=ALU.add)
            V("pool", "tensor_tensor", [g2, gg], w=[g2], out=g2[:, :], in0=g2[:, :], in1=gg[:, :], op=ALU.mult)
            V("act", "activation", [g2], w=[g2], out=g2[:, :], in_=g2[:, :], func=AF.Sigmoid, scale=1.5957691216057308)
            V("pool", "tensor_tensor", [g2, gg], w=[g2], out=g2[:, :], in0=g2[:, :], in1=gg[:, :], op=ALU.mult)
            V("dve", "tensor_tensor", [g2, ht], w=[g2], out=g2[:, :], in0=g2[:, :], in1=ht[:, :], op=ALU.mult)
            P.dma(yl[tcn], yl[tcn][c * 128:(c + 1) * 128, :], g2, g2[:, :], q="pool", part=True)

    lru_steps = [(c, tcn) for c in range(2) for tcn in range(S // TC)]

    kTa = P.sb("kTa", [96, S], BF16)
    qTa = P.sb("qTa", [96, S], BF16)
    Va = P.sb("Va", [128, 64, 65], BF16)
    st = [P.sb(f"st{i}", [64, TC], F32) for i in range(2)]
    sw = P.sb("sw", [16, TC], F32)
    Cc = P.sb("Cc", [16, TC], F32)
    Sc = P.sb("Sc", [16, TC], F32)
    cenT = P.sb("cenT", [64, 32], F32)
    gm = P.sb("gm", [128, 32], F32)
    m8 = P.sb("m8", [128, 8], F32)
    bf = P.sb("bf", [128, 32], F32)
    vst = [P.sb(f"vst{i}", [128, 16, 64], F32) for i in range(2)]
    ptile = [P.sb(f"pt{i}", [128, 512], BF16) for i in range(3)]
    osb = P.sb("osb", [128, 4, 65], F32)
    rden = P.sb("rden", [128, 4, 1], F32)
    yt = [P.sb(f"yt{i}", [128, 4, 64], F32) for i in range(2)]
    ps = [P.ps(f"ps{i}", [128, 512]) for i in range(3)]
    po = [P.ps(f"po{i}", [128, 512]) for i in range(2)]
    pg = P.ps("pg", [128, 512])
    nst = 0
    V("pool", "memset", [], w=[Va], ap=Va[:, :, 64:65], constant=1.0)
    nq = 0
    for h in range(4):
        P.dma(kTa, kTa[64:96, :], onehot, onehot[:, :], part=True)
        for vc in range(4):
            vs_ = vst[vc % 2]
            P.dma(vs_, vs_[:, :, :], vd, vd[vc * 2048:(vc + 1) * 2048, h * 64:(h + 1) * 64].rearrange("(t p) d -> p t d", p=128))
            V("pool", "tensor_copy", [vs_], wp=[Va], out=Va[:, vc * 16:(vc + 1) * 16, 0:64], in_=vs_[:, :, :])

        def rope_chunk(src, tcn, dstA):
            nonlocal nst
            s_ = st[nst % 2]
            nst += 1
            t0 = tcn * TC
            P.dma(s_, s_[:, :], uF, uF[src + h * 64:src + (h + 1) * 64, t0:t0 + TC])
            P.dma(sw, sw[0:8, :], uF, uF[src + h * 64 + 8:src + h * 64 + 16, t0:t0 + TC], part=True)
            P.dma(sw, sw[8:16, :], uF, uF[src + h * 64:src + h * 64 + 8, t0:t0 + TC], part=True)
            P.dma(Cc, Cc[:, :], ropeC, ropeC[:, t0:t0 + TC])
            P.dma(Sc, Sc[:, :], ropeS, ropeS[:, t0:t0 + TC])
            V("pool", "tensor_tensor", [sw, Sc], w=[sw], out=sw[:, :], in0=sw[:, :], in1=Sc[:, :], op=ALU.mult)
            V("dve", "tensor_tensor", [s_, Cc], w=[s_], out=s_[0:16, :], in0=s_[0:16, :], in1=Cc[:, :], op=ALU.mult)
            V("dve", "tensor_tensor", [s_, sw], w=[s_], out=s_[0:16, :], in0=s_[0:16, :], in1=sw[:, :], op=ALU.add)
            V("act", "copy", [s_], wp=[dstA], out=dstA[0:64, t0:t0 + TC], in_=s_[:, :])
            return s_

        for tcn in range(S // TC):
            s_ = rope_chunk(768, tcn, kTa)
            V("dve", "tensor_reduce", [s_], wp=[cenT], out=cenT[:, tcn * 8:(tcn + 1) * 8], in_=s_[:, :].rearrange("p (n k) -> p n k", k=256),
              axis=AX.X, op=ALU.add)
        for tcn in range(S // TC):
            s_ = rope_chunk(512, tcn, qTa)
            for qi in range(TC // 128):
                qt = tcn * (TC // 128) + qi
                qb = qt // 2
                P.op("pe", lambda s_=s_, qi=qi: nc.tensor.matmul(pg[:, 0:32], lhsT=s_[:, qi * 128:(qi + 1) * 128], rhs=cenT[:, :], start=True, stop=True),
                     r=[s_, cenT], w=[pg])
                V("dve", "tensor_tensor", [pg, past], w=[gm], out=gm[:, :], in0=pg[:, 0:32], in1=past[:, qb * 32:(qb + 1) * 32], op=ALU.add)
                V("dve", "max", [gm], w=[m8], out=m8[:, :], in_=gm[:, :])
                V("dve", "tensor_scalar_max", [m8], w=[m8], out=m8[:, 2:3], in0=m8[:, 2:3], scalar1=-1e29)
                V("dve", "scalar_tensor_tensor", [gm, m8, own], w=[bf], out=bf[:, :], in0=gm[:, :], scalar=m8[:, 2:3], in1=own[:, qb * 32:(qb + 1) * 32],
                  op0=ALU.is_ge, op1=ALU.add)
                V("dve", "tensor_scalar", [bf], w=[bf], out=bf[:, :], in0=bf[:, :], scalar1=-NEGB, scalar2=NEGB, op0=ALU.mult, op1=ALU.add)
                P.op("pe", lambda: nc.tensor.transpose(pg[0:32, 128:256], bf[:, :], ident_f[:, :]), r=[bf, ident_f], w=[pg])
                V("act", "copy", [pg], wp=[qTa], out=qTa[64:96, qt * 128:(qt + 1) * 128], in_=pg[0:32, 128:256])
        inflight = []

        def issue0(g, kt, h=h):
            nonlocal nq
            pp = ps[nq % 3]
            pt = ptile[nq % 3]
            nq += 1
            P.op("pe", lambda: nc.tensor.matmul(pp[:, :], lhsT=kTa[:, kt * 128:(kt + 1) * 128], rhs=qTa[:, g * 512:(g + 1) * 512],
                                                start=True, stop=True), r=[kTa, qTa], w=[pp])
            return (g, kt, pp, pt)

        def finish0(g, kt, pp, pt, h=h):
            pacc = po[g % 2]
            V("act", "activation", [pp], w=[pt], out=pt[:, :], in_=pp[:, :], func=AF.Exp, scale=0.125)
            j = kt - 4 * g
            if j >= 0:
                V("pool" if kt % 2 else "dve", "tensor_tensor", [pt, cm], w=[pt], out=pt[:, :], in0=pt[:, :], in1=cm[:, j * 512:(j + 1) * 512], op=ALU.mult)
            for qt in range(4):
                last = 4 * g + qt
                if kt > last:
                    continue
                P.op("pe", lambda qt=qt, last=last: nc.tensor.matmul(pacc[:, qt * 65:(qt + 1) * 65], lhsT=pt[:, qt * 128:(qt + 1) * 128],
                                                                     rhs=Va[:, kt, :], start=(kt == 0 and qt == 0), stop=(kt == last)),
                     r=[pt, Va], w=[pacc] if (kt == 0 and qt == 0) else (), wp=() if (kt == 0 and qt == 0) else [pacc])
            if kt == 4 * g + 3:
                V("dve", "tensor_copy", [pacc], w=[osb], out=osb[:, :, :], in_=pacc[:, 0:260].rearrange("p (t d) -> p t d", d=65))
                V("dve", "reciprocal", [osb], w=[rden], out=rden[:, :, :], in_=osb[:, :, 64:65])
                y_ = yt[g % 2]
                for qt in range(4):
                    V("pool", "tensor_scalar", [osb, rden], wp=[y_], out=y_[:, qt, :], in0=osb[:, qt, 0:64], scalar1=rden[:, qt, :], scalar2=None, op0=ALU.mult)
                P.dma(ya, ya[g * 512:(g + 1) * 512, h * 64:(h + 1) * 64].rearrange("(t p) d -> p t d", p=128), y_, y_[:, :, :], q="pool", part=True)

        for g in range(S // 512):
            for kt in range(4 * g + 4):
                inflight.append(issue0(g, kt))
                if len(inflight) > 2:
                    finish0(*inflight.pop(0))
            if g % 2 == 1 and lru_steps:
                lru_step(*lru_steps.pop(0))
        while inflight:
            finish0(*inflight.pop(0))
        while lru_steps:
            lru_step(*lru_steps.pop(0))
    P.emit()


S = 8192
TC = 2048
NEGB = -30000.0
NH = 8


def mx1_phase(P, io):
    nc = P.nc

    def V(eng, name, r, w=(), wp=(), **kw):
        e = P._eng(eng)
        P.op(eng, lambda: getattr(e, name)(**kw), r=r, w=w, wp=wp)

    u1F, t1F, posd, w1d, w2d, ropeC, ropeS, OHd, cmd, wld, cmkd, ovd, selbd, identd, yd = (io[k] for k in (
        "u1F", "t1F", "cpos", "cw1", "cw2", "ropeC", "ropeS", "OH", "cm", "wl", "cmask", "ov", "selbase", "ident", "y1F"))

    ident_f = P.sb("identf", [128, 128], F32)
    cm = P.sb("cm", [128, 4 * 512], BF16)
    wl = P.sb("wl", [128, 4 * 512], BF16)
    cmask = P.sb("cmask", [128, 5 * 512], BF16)
    ov = P.sb("ov", [128, 4 * 128], BF16)
    for (t, d) in ((ident_f, identd), (cm, cmd), (wl, wld), (cmask, cmkd), (ov, ovd)):
        P.dma(t, t[:, :], d, d[:, :])

    ksT = P.sb("ksT", [128, S], BF16)
    P.dma(ksT, ksT[64:128, :], OHd, OHd[:, :], part=True)
    kwT = P.sb("kwT", [64, S], BF16)
    vsa = P.sb("vsa", [128, 64, 65], BF16)
    vwa = P.sb("vwa", [128, 64, 65], BF16)
    kcmpT = P.sb("kcmpT", [64, 512], BF16)
    vcmp = P.sb("vcmp", [128, 4, 65], BF16)
    st = [P.sb(f"st{i}", [64, TC + 16], F32) for i in range(2)]
    sw = P.sb("sw", [16, 4, 512], F32)
    Cg = P.sb("Cg", [16, 512], F32)
    Sg = P.sb("Sg", [16, 512], F32)
    vst = [P.sb(f"vst{i}", [128, 16, 64], F32) for i in range(2)]
    w1s = P.sb("w1s", [64, 32, 128], F32)
    w2s = P.sb("w2s", [128, 64], F32)
    posT = P.sb("posT", [64, 32], F32)
    hid = P.sb("hid", [128, 512], F32)
    hz = P.sb("hz", [128, 512], F32)
    pbias = P.sb("pbias", [128, 1], F32)

    ps = [P.ps(f"ps{i}", [128, 512]) for i in range(3)]
    po_c = P.ps("po_c", [128, 512])
    pimp = P.ps("pimp", [128, 512])
    po_s = P.ps("po_s", [128, 512])
    po_w = P.ps("po_w", [128, 512])
    pmisc = P.ps("pmisc", [128, 512])
    nst = 0

    V("pool", "memset", [], w=[vsa], ap=vsa[:, :, 64:65], constant=1.0)
    V("pool", "memset", [], w=[vwa], ap=vwa[:, :, 64:65], constant=1.0)
    V("pool", "memset", [], w=[vcmp], ap=vcmp[:, :, 64:65], constant=1.0)
    nv = 0
    for (vcol, dst) in ((0, vsa), (64, vwa)):
        for vc in range(4):
            vs_ = vst[nv % 2]
            nv += 1
            P.dma(vs_, vs_[:, :, :], t1F, t1F[vc * 2048:(vc + 1) * 2048, vcol:vcol + 64].rearrange("(t p) d -> p t d", p=128))
            V("pool", "tensor_copy", [vs_], wp=[dst], out=dst[:, vc * 16:(vc + 1) * 16, 0:64], in_=vs_[:, :, :])

    for (rb, dstA) in ((640, ksT), (704, kwT)):
        for tcn in range(S // TC):
            s_ = st[nst % 2]
            nst += 1
            t0 = tcn * TC
            P.dma(s_, s_[:, 0:TC], u1F, u1F[rb:rb + 64, t0:t0 + TC])
            swv = sw[:, :, :].rearrange("p a b -> p (a b)")
            P.dma(sw, swv[0:8, :], u1F, u1F[rb + 8:rb + 16, t0:t0 + TC], part=True)
            P.dma(sw, swv[8:16, :], u1F, u1F[rb:rb + 8, t0:t0 + TC], part=True)
            for sub in range(4):
                c0 = sub * 512
                P.dma(Cg, Cg[:, :], ropeC, ropeC[:, t0 + c0:t0 + c0 + 512])
                P.dma(Sg, Sg[:, :], ropeS, ropeS[:, t0 + c0:t0 + c0 + 512])
                V("pool", "tensor_tensor", [sw, Sg], w=[sw], out=sw[:, sub, :], in0=sw[:, sub, :], in1=Sg[:, :], op=ALU.mult)
                V("dve", "tensor_tensor", [s_, Cg], w=[s_], out=s_[0:16, c0:c0 + 512], in0=s_[0:16, c0:c0 + 512], in1=Cg[:, :], op=ALU.mult)
                V("dve", "tensor_tensor", [s_, sw], w=[s_], out=s_[0:16, c0:c0 + 512], in0=s_[0:16, c0:c0 + 512], in1=sw[:, sub, :], op=ALU.add)
            V("act", "copy", [s_], wp=[dstA], out=dstA[0:64, t0:t0 + TC], in_=s_[:, 0:TC])

    def gelu_inplace(z, tmp):
        V("pool", "tensor_tensor", [z], w=[tmp], out=tmp[:, :], in0=z[:, :], in1=z[:, :], op=ALU.mult)
        V("pool", "tensor_scalar", [tmp], w=[tmp], out=tmp[:, :], in0=tmp[:, :], scalar1=0.044715, scalar2=1.0, op0=ALU.mult, op1=ALU.add)
        V("pool", "tensor_tensor", [tmp, z], w=[tmp], out=tmp[:, :], in0=tmp[:, :], in1=z[:, :], op=ALU.mult)
        V("act", "activation", [tmp], w=[tmp], out=tmp[:, :], in_=tmp[:, :], func=AF.Sigmoid, scale=1.5957691216057308)
        V("pool", "tensor_tensor", [tmp, z], w=[z], out=z[:, :], in0=tmp[:, :], in1=z[:, :], op=ALU.mult)

    for idx, rb in ((0, 512), (1, 576)):
        P.dma(w1s, w1s[:, :, :], w1d, w1d[idx].rearrange("(j d) m -> d j m", d=64))
        P.dma(w2s, w2s[:, :], w2d, w2d[idx])
        P.dma(posT, posT[:, :], posd, posd[idx].rearrange("j d -> d j"), allow_slow_non_contiguous=True)
        for j in range(32):
            P.op("pe", lambda j=j: nc.tensor.matmul(pmisc[:, 0:1], lhsT=w1s[:, j, :], rhs=posT[:, j:j + 1], start=(j == 0), stop=(j == 31)),
                 r=[w1s, posT], w=[pmisc] if j == 0 else (), wp=[pmisc] if j else ())
        V("dve", "tensor_copy", [pmisc], w=[pbias], out=pbias[:, :], in_=pmisc[:, 0:1])
        V("pool", "memset", [], wp=[hid], ap=hid[:, 511:512], constant=0.0)
        for cc in range(4):
            n = 128 if cc < 3 else 127
            s_ = st[nst % 2]
            nst += 1
            ntok = 2048 + 16 if cc < 3 else 2048
            P.dma(s_, s_[:, 0:ntok], u1F, u1F[rb:rb + 64, cc * 2048:cc * 2048 + ntok])
            sv = s_[:, 0:2048 + 16].rearrange("p (i s) -> p i s", s=16)
            pp = ps[cc % 3]
            for j in range(32):
                P.op("pe", lambda j=j, pp=pp, sv=sv, n=n: nc.tensor.matmul(pp[:, 0:n], lhsT=w1s[:, j, :], rhs=sv[:, (j // 16):(j // 16) + n, j % 16],
                                                                         start=(j == 0), stop=(j == 31)),
                     r=[w1s, s_], w=[pp] if j == 0 else (), wp=[pp] if j else ())
            V("dve", "tensor_scalar", [pp, pbias], wp=[hid], out=hid[:, cc * 128:cc * 128 + n], in0=pp[:, 0:n], scalar1=pbias[:, 0:1], scalar2=None, op0=ALU.add)
        gelu_inplace(hid, hz)
        if idx == 0:
            P.op("pe", lambda: nc.tensor.matmul(pmisc[0:64, :], lhsT=w2s[:, :], rhs=hid[:, :], start=True, stop=True), r=[w2s, hid], w=[pmisc])
            V("act", "copy", [pmisc], w=[kcmpT], out=kcmpT[:, :], in_=pmisc[0:64, :])
        else:
            for cc in range(4):
                P.op("pe", lambda cc=cc: nc.tensor.matmul(pmisc[:, cc * 64:(cc + 1) * 64], lhsT=hid[:, cc * 128:(cc + 1) * 128], rhs=w2s[:, :],
                                                         start=(cc == 0), stop=(cc == 3)), r=[w2s, hid], w=[pmisc] if cc == 0 else (), wp=[pmisc] if cc else ())
            V("act", "copy", [pmisc], wp=[vcmp], out=vcmp[:, :, 0:64], in_=pmisc[:, 0:256].rearrange("p (c d) -> p c d", d=64))

    qst = P.sb("qst", [64, NH, 512], F32)
    qn = P.sb("qn", [64, NH, 512], BF16)
    qa = [P.sb(f"qa{i}", [128, NH, 512], BF16) for i in range(2)]
    qr = qa[0]
    ptile = [P.sb(f"pt{i}", [128, 512], BF16) for i in range(3)]
    oc_sb = [P.sb(f"oc{r}", [128, 4, 65], F32) for r in range(NH)]
    os_sb = P.sb("os_sb", [128, 4, 65], F32)
    ow_sbs = [P.sb(f"ow_sb{r}", [128, 4, 65], F32) for r in range(NH)]
    impacc = P.sb("impacc", [128, 4, 128], F32)
    selb = P.sb("selb", [128, 4, 128], F32)
    tmpi = P.sb("tmpi", [128, 128], F32)
    m8 = P.sb("m8", [128, 16], F32)
    biasT = P.sb("biasT", [128, 512], BF16)
    glt = P.sb("glt", [128, 4, 24], F32)
    cf = P.sb("cf", [128, 3, 4], F32)
    yt = [P.sb(f"yt{i}", [128, 4, NH * 64], F32) for i in range(1)]
    nq = 0

    pipe = []

    def push(mm_list, maskap, fin):
        nonlocal nq
        pp = ps[nq % 3]
        pt = ptile[nq % 3]
        par = nq % 2
        nq += 1
        for i, (l, r_, tl) in enumerate(mm_list):
            P.op("pe", lambda l=l, r_=r_, pp=pp, i=i: nc.tensor.matmul(pp[:, :], lhsT=l, rhs=r_, start=(i == 0), stop=(i == len(mm_list) - 1)),
                 r=tl, w=[pp] if i == 0 else (), wp=[pp] if i else ())
        pipe.append((pp, pt, par, maskap, fin))
        if len(pipe) > 2:
            pop()

    def pop():
        pp, pt, par, maskap, fin = pipe.pop(0)
        V("act", "activation", [pp], w=[pt], out=pt[:, :], in_=pp[:, :], func=AF.Exp, scale=0.125)
        if maskap is not None:
            mt, map_ = maskap
            V("pool" if par else "dve", "tensor_tensor", [pt, mt], w=[pt], out=pt[:, :], in0=pt[:, :], in1=map_, op=ALU.mult)
        fin(pt)

    def flush():
        while pipe:
            pop()

    for g in range(S // 512):
        P.dma(qst, qst[:, :, :], u1F, u1F[0:512, g * 512:(g + 1) * 512].rearrange("(r d) t -> d r t", d=64))
        for hf in range(2):
            P.dma(sw, sw[0:8, :, :], u1F, u1F[0:512, g * 512:(g + 1) * 512].rearrange("(r d) t -> d r t", d=64)[8:16, hf * 4:(hf + 1) * 4, :], part=True)
            P.dma(sw, sw[8:16, :, :], u1F, u1F[0:512, g * 512:(g + 1) * 512].rearrange("(r d) t -> d r t", d=64)[0:8, hf * 4:(hf + 1) * 4, :], part=True)
            if hf == 0:
                P.dma(Cg, Cg[:, :], ropeC, ropeC[:, g * 512:(g + 1) * 512])
                P.dma(Sg, Sg[:, :], ropeS, ropeS[:, g * 512:(g + 1) * 512])
                V("act", "copy", [qst], w=[qn], out=qn[:, :, :], in_=qst[:, :, :])
            for r4 in range(4):
                r = hf * 4 + r4
                V("pool", "tensor_tensor", [sw, Sg], w=[sw], out=sw[:, r4, :], in0=sw[:, r4, :], in1=Sg[:, :], op=ALU.mult)
                V("dve", "tensor_tensor", [qst, Cg], w=[qst], out=qst[0:16, r, :], in0=qst[0:16, r, :], in1=Cg[:, :], op=ALU.mult)
                V("dve", "tensor_tensor", [qst, sw], w=[qst], out=qst[0:16, r, :], in0=qst[0:16, r, :], in1=sw[:, r4, :], op=ALU.add)
        V("act", "copy", [qst], wp=[qa[0]], out=qa[0][0:64, :, :], in_=qst[:, :, :])
        V("pool", "tensor_copy", [qst], wp=[qa[1]], out=qa[1][0:64, :, :], in_=qst[:, :, :])
        P.dma(glt, glt[:, :, :], t1F, t1F[g * 512:(g + 1) * 512, 128:152].rearrange("(t p) c -> p t c", p=128))
        V("act", "activation", [glt], w=[glt], out=glt[:, :, :], in_=glt[:, :, :], func=AF.Sigmoid)
        P.dma(selb, selb[:, :, :], selbd, selbd[g * 512:(g + 1) * 512, :].rearrange("(t p) n -> p t n", p=128))
        ccs = [cc for cc in range(4) if 2048 * cc - 512 * g < 512]
        for r in range(NH):
            for ci, cc in enumerate(ccs):
                off = 2048 * cc - 512 * g
                mk = None
                if off > -2560:
                    m = (off + 2048) // 512
                    mk = (cmask, cmask[:, m * 512:(m + 1) * 512])
                lastc = (ci == len(ccs) - 1)

                def finA(pt, r=r, cc=cc, ci=ci, lastc=lastc):
                    for qt in range(4):
                        first = (ci == 0 and qt == 0)
                        P.op("pe", lambda qt=qt, first=first: nc.tensor.matmul(po_c[:, qt * 65:(qt + 1) * 65], lhsT=pt[:, qt * 128:(qt + 1) * 128],
                                                                               rhs=vcmp[:, cc, :], start=first, stop=lastc),
                             r=[pt, vcmp], w=[po_c] if first else (), wp=() if first else [po_c])
                        P.op("pe", lambda qt=qt, first=first: nc.tensor.matmul(pimp[:, qt * 128:(qt + 1) * 128], lhsT=pt[:, qt * 128:(qt + 1) * 128],
                                                                               rhs=ov[:, cc * 128:(cc + 1) * 128], start=first, stop=lastc),
                             r=[pt, ov], w=[pimp] if first else (), wp=() if first else [pimp])
                    if not lastc:
                        return
                    oc = oc_sb[r]
                    V("dve", "tensor_copy", [po_c], w=[oc], out=oc[:, :, :], in_=po_c[:, 0:260].rearrange("p (t d) -> p t d", d=65))
                    V("dve", "tensor_scalar_max", [oc], wp=[cf], out=cf[:, 0, :], in0=oc[:, :, 64], scalar1=1e-30)
                    V("dve", "reciprocal", [cf], w=[cf], out=cf[:, 0, :], in_=cf[:, 0, :])
                    for qt in range(4):
                        if r == 0:
                            V("dve", "tensor_scalar", [pimp, cf], wp=[impacc], out=impacc[:, qt, :], in0=pimp[:, qt * 128:(qt + 1) * 128], scalar1=cf[:, 0, qt:qt + 1],
                              scalar2=None, op0=ALU.mult)
                        else:
                            V("dve", "scalar_tensor_tensor", [pimp, cf, impacc], w=[impacc], out=impacc[:, qt, :], in0=pimp[:, qt * 128:(qt + 1) * 128],
                              scalar=cf[:, 0, qt:qt + 1], in1=impacc[:, qt, :], op0=ALU.mult, op1=ALU.add)
                    V("dve", "tensor_tensor", [cf, glt], w=[oc], out=oc[:, :, 64], in0=cf[:, 0, :], in1=glt[:, :, 3 * r + 0], op=ALU.mult)

                push([(kcmpT[:, cc * 128:(cc + 1) * 128], qn[:, r, :], [kcmpT, qn])], mk, finA)
        for r in range(NH):
            kts = list(range(max(0, 4 * g - 4), 4 * g + 4))
            for kt in kts:
                j = kt - 4 * g
                mk = (cm, cm[:, j * 512:(j + 1) * 512]) if j >= 0 else (wl, wl[:, (j + 4) * 512:(j + 5) * 512])

                def finW(pt, r=r, kt=kt, j=j, firstkt=(kt == kts[0]), lastkt=(kt == kts[-1])):
                    firstw = firstkt
                    for qt in range(4):
                        if not (qt - 4 <= j <= qt):
                            continue
                        P.op("pe", lambda qt=qt, firstw=firstw: nc.tensor.matmul(po_w[:, qt * 65:(qt + 1) * 65], lhsT=pt[:, qt * 128:(qt + 1) * 128],
                                                                                 rhs=vwa[:, kt, :], start=firstw, stop=(j == qt)),
                             r=[pt, vwa], w=[po_w] if firstw else (), wp=() if firstw else [po_w])
                        firstw = False
                    if lastkt:
                        V("dve", "tensor_copy", [po_w], w=[ow_sbs[r]], out=ow_sbs[r][:, :, :], in_=po_w[:, 0:260].rearrange("p (t d) -> p t d", d=65))

                push([(kwT[:, kt * 128:(kt + 1) * 128], qa[0][0:64, r, :], [kwT, qa[0]])], mk, finW)
        flush()
        for qt in range(4):
            V("dve", "tensor_tensor", [impacc, selb], w=[impacc], out=impacc[:, qt, :], in0=impacc[:, qt, :], in1=selb[:, qt, :], op=ALU.add)
            V("dve", "max", [impacc], w=[m8], out=m8[:, 0:8], in_=impacc[:, qt, :])
            V("dve", "match_replace", [impacc, m8], w=[tmpi], out=tmpi[:, :], in_to_replace=m8[:, 0:8], in_values=impacc[:, qt, :], imm_value=-3e38)
            V("dve", "max", [tmpi], w=[m8], out=m8[:, 8:16], in_=tmpi[:, :])
            V("dve", "tensor_scalar_max", [m8], w=[m8], out=m8[:, 15:16], in0=m8[:, 15:16], scalar1=-1e29)
            V("dve", "tensor_scalar", [impacc, m8], w=[tmpi], out=tmpi[:, :], in0=impacc[:, qt, :], scalar1=m8[:, 15:16], scalar2=None, op0=ALU.is_ge)
            V("dve", "tensor_scalar", [tmpi], w=[tmpi], out=tmpi[:, :], in0=tmpi[:, :], scalar1=-NEGB, scalar2=NEGB, op0=ALU.mult, op1=ALU.add)
            P.op("pe", lambda: nc.tensor.transpose(pmisc[:, 0:128], tmpi[:, :], ident_f[:, :]), r=[tmpi, ident_f], w=[pmisc])
            V("act", "copy", [pmisc], wp=[biasT], out=biasT[:, qt * 128:(qt + 1) * 128], in_=pmisc[:, 0:128])
        for r in range(NH):
            V("act" if r % 2 else "pool", "copy" if r % 2 else "tensor_copy", [biasT], wp=[qa[0]], out=qa[0][64:128, r, :], in_=biasT[0:64, :])
            V("pool" if r % 2 else "act", "tensor_copy" if r % 2 else "copy", [biasT], wp=[qa[1]], out=qa[1][64:128, r, :], in_=biasT[64:128, :])
        y_ = yt[0]
        for r in range(NH):
            nkt = 4 * g + 4
            for kt in range(nkt):
                j = kt - 4 * g
                mk = (cm, cm[:, j * 512:(j + 1) * 512]) if j >= 0 else None

                def finS(pt, r=r, kt=kt, nkt=nkt):
                    for qt in range(4):
                        last = 4 * g + qt
                        if kt > last:
                            continue
                        first = (kt == 0 and qt == 0)
                        P.op("pe", lambda qt=qt, first=first, last=last: nc.tensor.matmul(po_s[:, qt * 65:(qt + 1) * 65], lhsT=pt[:, qt * 128:(qt + 1) * 128],
                                                                                         rhs=vsa[:, kt, :], start=first, stop=(kt == last)),
                             r=[pt, vsa], w=[po_s] if first else (), wp=() if first else [po_s])
                    if kt == nkt - 1:
                        V("dve", "tensor_copy", [po_s], w=[os_sb], out=os_sb[:, :, :], in_=po_s[:, 0:260].rearrange("p (t d) -> p t d", d=65))
                        oc = oc_sb[r]
                        ow_sb = ow_sbs[r]
                        for bi, osb_ in ((1, os_sb), (2, ow_sb)):
                            V("dve", "tensor_scalar_max", [osb_], wp=[cf], out=cf[:, bi, :], in0=osb_[:, :, 64], scalar1=1e-30)
                            V("dve", "reciprocal", [cf], w=[cf], out=cf[:, bi, :], in_=cf[:, bi, :])
                            V("dve", "tensor_tensor", [cf, glt], w=[cf], out=cf[:, bi, :], in0=cf[:, bi, :], in1=glt[:, :, 3 * r + bi], op=ALU.mult)
                        for qt in range(4):
                            yv = y_[:, qt, r * 64:(r + 1) * 64]
                            V("dve", "tensor_scalar", [oc], wp=[y_], out=yv, in0=oc[:, qt, 0:64], scalar1=oc[:, qt, 64:65], scalar2=None, op0=ALU.mult)
                            V("dve", "scalar_tensor_tensor", [os_sb, cf, y_], w=[y_], out=yv, in0=os_sb[:, qt, 0:64], scalar=cf[:, 1, qt:qt + 1], in1=yv, op0=ALU.mult, op1=ALU.add)
                            V("dve", "scalar_tensor_tensor", [ow_sb, cf, y_], w=[y_], out=yv, in0=ow_sb[:, qt, 0:64], scalar=cf[:, 2, qt:qt + 1], in1=yv, op0=ALU.mult, op1=ALU.add)

                push([(ksT[:, kt * 128:(kt + 1) * 128], qa[kt // 32][:, r, :], [ksT, qa[kt // 32]])], mk, finS)
        flush()
        P.dma(yd, yd[g * 512:(g + 1) * 512, :].rearrange("(t p) d -> p t d", p=128), y_, y_[:, :, :], q="pool", part=True)
    P.emit()


def mx0_consts():
    half = 8
    freqs = 500000.0 ** (-np.arange(half, dtype=np.float32) * 2.0 / 16)
    ang = np.arange(S, dtype=np.float32)[:, None] * freqs[None, :]
    cos = np.cos(ang).astype(np.float32).T
    sin = np.sin(ang).astype(np.float32).T
    ropeC = np.concatenate([cos, cos], 0)
    ropeS = np.concatenate([-sin, sin], 0)
    onehot = (np.arange(S)[None, :] // 256 == np.arange(32)[:, None]).astype(ml_dtypes.bfloat16)
    p = np.arange(128)[:, None]
    f = np.arange(512)[None, :]
    cm = np.concatenate([((128 * j + p) <= f) for j in range(4)], axis=1).astype(ml_dtypes.bfloat16)
    qb = np.arange(32)[:, None]
    n = np.arange(32)[None, :]
    past = np.where(n < qb, 0.0, -1e30).astype(np.float32).reshape(1, -1).repeat(128, 0)
    own = (n == qb).astype(np.float32).reshape(1, -1).repeat(128, 0)
    return dict(ropeC=np.ascontiguousarray(ropeC), ropeS=np.ascontiguousarray(ropeS), onehot=onehot, cm=np.ascontiguousarray(cm),
                past=np.ascontiguousarray(past), own=np.ascontiguousarray(own), ident=np.eye(128, dtype=np.float32))


def mx1_consts():
    half = 8
    freqs = 500000.0 ** (-np.arange(half, dtype=np.float32) * 2.0 / 16)
    ang = np.arange(S, dtype=np.float32)[:, None] * freqs[None, :]
    cos = np.cos(ang).astype(np.float32).T
    sin = np.sin(ang).astype(np.float32).T
    bf = ml_dtypes.bfloat16
    p = np.arange(128)[:, None]; f = np.arange(512)[None, :]
    cm = np.concatenate([((128 * j + p) <= f) for j in range(4)], axis=1).astype(bf)
    wl = np.concatenate([(f < (128 * jj + p)) for jj in range(4)], axis=1).astype(bf)
    cmask = np.concatenate([(f >= 16 * p + 31 + (-2048 + 512 * m)) for m in range(5)], axis=1).astype(bf)
    OH = ((np.arange(S)[None, :] // 64) % 64 == np.arange(64)[:, None]).astype(bf)
    ncmp = 511
    ci = np.arange(ncmp)[:, None] * 16; sj = np.arange(128)[None, :] * 64
    overlap = np.clip(np.minimum(ci + 32, sj + 64) - np.maximum(ci, sj), 0, None).astype(np.float32) / 32
    ovp = np.zeros((512, 128), np.float32); ovp[:511] = overlap
    ov = ovp.reshape(4, 128, 128).transpose(1, 0, 2).reshape(128, 512).astype(bf)
    t = np.arange(S)[:, None]; jn = np.arange(128)[None, :]
    blk = t // 64
    forced = (jn == 0) | (jn == blk) | (jn == blk - 1)
    selbase = np.where(jn > blk, -1e30, np.where(forced, 1e30, 0.0)).astype(np.float32)
    return dict(ropeC=np.ascontiguousarray(np.concatenate([cos, cos], 0)), ropeS=np.ascontiguousarray(np.concatenate([-sin, sin], 0)),
                OH=OH, cm=np.ascontiguousarray(cm), wl=np.ascontiguousarray(wl), cmask=np.ascontiguousarray(cmask), ov=np.ascontiguousarray(ov),
                selbase=np.ascontiguousarray(selbase), ident=np.eye(128, dtype=np.float32))


PAIRS = [[0, 1], [2, 3], [4, 5], [6, 7]]


def sel_phase(P, sel_d, fm_items, tm_items, gathers):
    nc = P.nc
    for (i_t, i_ap, o_t) in gathers:
        P.allgather(i_t, o_t, PAIRS, in_ap=i_ap)
    sel = P.sb("sel", [128, 2], F32)
    P.dma(sel, sel[:, :], sel_d, sel_d[:, :])
    bufs = [(P.sb(f"sa{i}", [128, 2048], F32), P.sb(f"sb{i}", [128, 2048], F32)) for i in range(3)]
    k = 0

    def blend(ta, tb, apa, apb, wa, wb, rn=128):
        P.op("act", lambda: nc.scalar.activation(out=apa, in_=apa, func=AF.Copy, scale=sel[0:rn, wa:wa + 1]), r=[ta, sel], w=[ta])
        P.op("dve", lambda: nc.vector.scalar_tensor_tensor(out=apa, in0=apb, scalar=sel[0:rn, wb:wb + 1], in1=apa, op0=ALU.mult, op1=ALU.add),
             r=[ta, tb, sel], w=[ta])

    for (dst, dr, dc, A, ar, ac, B, br, bc, nr, ncol, wa, wb) in fm_items:
        for r0 in range(0, nr, 128):
            rn = min(128, nr - r0)
            for c0 in range(0, ncol, 2048):
                cw = min(2048, ncol - c0)
                ta, tb = bufs[k % 3]
                k += 1
                P.dma(ta, ta[0:rn, 0:cw], A, A[ar + r0:ar + r0 + rn, ac + c0:ac + c0 + cw])
                P.dma(tb, tb[0:rn, 0:cw], B, B[br + r0:br + r0 + rn, bc + c0:bc + c0 + cw])
                blend(ta, tb, ta[0:rn, 0:cw], tb[0:rn, 0:cw], wa, wb, rn)
                P.dma(dst, dst[dr + r0:dr + r0 + rn, dc + c0:dc + c0 + cw], ta, ta[0:rn, 0:cw], q="pool", part=True)
    for (dst, dr, dc, A, ar, ac, B, br, bc, nr, ncol, wa, wb) in tm_items:
        tt = 1
        while tt * 2 * ncol <= 2048 and (nr // 128) % (tt * 2) == 0:
            tt *= 2
        step = 128 * tt
        assert nr % step == 0
        for r0 in range(0, nr, step):
            ta, tb = bufs[k % 3]
            k += 1
            va = ta[:, 0:tt * ncol].rearrange("p (t d) -> p t d", d=ncol)
            vb = tb[:, 0:tt * ncol].rearrange("p (t d) -> p t d", d=ncol)
            P.dma(ta, va, A, A[ar + r0:ar + r0 + step, ac:ac + ncol].rearrange("(t p) d -> p t d", p=128))
            P.dma(tb, vb, B, B[br + r0:br + r0 + step, bc:bc + ncol].rearrange("(t p) d -> p t d", p=128))
            blend(ta, tb, ta[:, 0:tt * ncol], tb[:, 0:tt * ncol], wa, wb)
            P.dma(dst, dst[dr + r0:dr + r0 + step, dc:dc + ncol].rearrange("(t p) d -> p t d", p=128), ta, va, q="pool", part=True)
    P.emit()


def build_fused(stop=99):
    P = Prog()
    S2 = 2 * NT
    di = {}

    def din(name, shape, dt=F32):
        di[name] = P.dram_in(name, shape, dt)
        return di[name]

    x = din("x", [NT, D]); c = din("c", [128, 8]); ident = din("ident", [128, 128]); sel = din("sel", [128, 2])
    modw = [din(f"modw{l}", [D, 9 * D]) for l in range(2)]
    modb = [din(f"modb{l}", [1, 9 * D]) for l in range(2)]
    ng = [din(f"ng{l}", [3, D]) for l in range(2)]
    w1 = [[din(f"w1_{l}{i}", [D, 2 * DFF]) for i in range(2)] for l in range(2)]
    w2 = [[din(f"w2_{l}{i}", [DFF, D]) for i in range(2)] for l in range(2)]
    win0 = din("win0", [D, 2560]); wout0 = din("wout0", [D, D]); win1 = din("win1", [D, 1840]); wout1 = din("wout1", [D, D])
    fg = din("fg", [1, D])
    for nm, shp, dt in (("lrup", [256, 8], F32), ("wa", [256, 128], F32), ("wx", [256, 128], F32), ("ropeC", [16, S], F32), ("ropeS", [16, S], F32),
                        ("onehot", [32, S], BF16), ("cm", [128, 2048], BF16), ("past", [128, 1024], F32), ("own", [128, 1024], F32),
                        ("cpos", [2, 32, 64], F32), ("cw1", [2, 2048, 128], F32), ("cw2", [2, 128, 64], F32), ("OH", [64, S], BF16),
                        ("wl", [128, 2048], BF16), ("cmask", [128, 2560], BF16), ("ov", [128, 512], BF16), ("selbase", [S, 128], F32)):
        din(nm, shp, dt)
    out = P.dram_out("out", [NT, D], F32)
    tmp = lambda n, shp: P.dram_tmp(n, shp, F32)
    x1, x3, x4, xs = tmp("x1", [NT, D]), tmp("x3", [NT, D]), tmp("x4", [NT, D]), tmp("xs", [NT, D])
    uA, uB = tmp("uA", [1024, NT]), tmp("uB", [1024, NT])
    gu = [tmp(f"gu{j}", [256, NT]) for j in range(8)]
    vA, vB = tmp("vA", [NT, 256]), tmp("vB", [NT, 256])
    gv = [tmp(f"gv{j}", [4096, 256]) for j in range(2)]
    uF, vF = tmp("uF", [1024, S2]), tmp("vF", [S2, 256])
    ylF, yaF = [tmp(f"ylF{j}", [256, 2048]) for j in range(4)], tmp("yaF", [S2, 256])
    gyl = [tmp(f"gyl{j}", [512, 2048]) for j in range(4)]
    gya = [tmp(f"gya{j}", [4096, 256]) for j in range(4)]
    yTs, ytms = tmp("yTs", [512, NT]), tmp("ytms", [NT, 512])
    u1A, u1B = tmp("u1A", [768, NT]), tmp("u1B", [768, NT])
    gu1 = [tmp(f"gu1{j}", [256, NT]) for j in range(6)]
    t1A, t1B = tmp("t1A", [NT, 152]), tmp("t1B", [NT, 152])
    gt1 = [tmp(f"gt1{j}", [4096, 152]) for j in range(2)]
    u1F, t1F = tmp("u1F", [768, S2]), tmp("t1F", [S2, 152])
    y1F, ytm1s = tmp("y1F", [S2, 512]), tmp("ytm1s", [NT, 1024])
    gy1 = [tmp(f"gy1{j}", [2048, 512]) for j in range(8)]

    base = dict(c=c, ident=ident)
    post1 = dict(norm_idx=1, mod_idx0=3, wc=2560,
                 fm=[(i * 128, uA, i * 128) for i in range(8)] + [(1280 + i * 128, uB, i * 128) for i in range(8)],
                 tm=[(1024, 256, vA, 0), (2304, 256, vB, 0)])
    tp_phase(P, dict(ffn=(0, 0), post=post1), dict(base, x_in=x, x_out=x1, modw=modw[0], modb=modb[0], ng=ng[0], w1=w1[0][0], w2=w2[0][0], win=win0))
    if stop <= 1:
        return P
    sel_phase(P, sel,
              fm_items=[it for j in range(8) for it in ((uF, 128 * j, 0, uA, 128 * j, 0, gu[j], 0, 0, 128, NT, 0, 1),
                                                        (uF, 128 * j, NT, uA, 128 * j, 0, gu[j], 128, 0, 128, NT, 1, 0))],
              tm_items=[it for j in range(2) for it in ((vF, 2048 * j, 0, vA, 2048 * j, 0, gv[j], 0, 0, 2048, 256, 0, 1),
                                                        (vF, NT + 2048 * j, 0, vA, 2048 * j, 0, gv[j], 2048, 0, 2048, 256, 1, 0))],
              gathers=[(uB, uB[128 * j:128 * (j + 1), :], gu[j]) for j in range(8)] + [(vB, vB[2048 * j:2048 * (j + 1), :], gv[j]) for j in range(2)])
    if stop <= 2:
        return P
    mx0_phase(P, dict(di, uF=uF, vF=vF, ylF=ylF, yaF=yaF))
    if stop <= 3:
        return P
    if stop <= 3:
        return P
    if stop <= 4:
        return P
    tp_phase(P, dict(pre=dict(nfm=4, tmw=512), ffn=(2, 6)),
             dict(base, x_in=x1, x_out=x3, modw=modw[0], modb=modb[0], ng=ng[0], w1=w1[0][1], w2=w2[0][1], wout=wout0, xs=xs, sel=sel,
                  gathers=[(ylF[j], None, gyl[j]) for j in range(4)] + [(yaF, yaF[2048 * j:2048 * (j + 1), :], gya[j]) for j in range(4)],
                  fm_cand=lambda tok0: (gyl[tok0 // 2048], gyl[tok0 // 2048][:, tok0 % 2048:tok0 % 2048 + 128],
                                        gyl[2 + tok0 // 2048], gyl[2 + tok0 // 2048][:, tok0 % 2048:tok0 % 2048 + 128]),
                  tm_cand=lambda tok0: [(256 * r, 256, gya[tok0 // 2048], gya[tok0 // 2048][2048 * r + tok0 % 2048:2048 * r + tok0 % 2048 + 128, :],
                                         gya[2 + tok0 // 2048], gya[2 + tok0 // 2048][2048 * r + tok0 % 2048:2048 * r + tok0 % 2048 + 128, :]) for r in range(2)]))
    if stop <= 5:
        return P
    post3 = dict(norm_idx=1, mod_idx0=3, wc=1840,
                 fm=[(i * 128, u1A, i * 128) for i in range(6)] + [(920 + i * 128, u1B, i * 128) for i in range(6)],
                 tm=[(768, 152, t1A, 0), (920 + 768, 152, t1B, 0)])
    tp_phase(P, dict(ffn=(0, 0), post=post3), dict(base, x_in=x3, x_out=x4, modw=modw[1], modb=modb[1], ng=ng[1], w1=w1[1][0], w2=w2[1][0], win=win1))
    if stop <= 6:
        return P
    sel_phase(P, sel,
              fm_items=[it for j in range(6) for it in ((u1F, 128 * j, 0, u1A, 128 * j, 0, gu1[j], 0, 0, 128, NT, 0, 1),
                                                        (u1F, 128 * j, NT, u1A, 128 * j, 0, gu1[j], 128, 0, 128, NT, 1, 0))],
              tm_items=[it for j in range(2) for it in ((t1F, 2048 * j, 0, t1A, 2048 * j, 0, gt1[j], 0, 0, 2048, 152, 0, 1),
                                                        (t1F, NT + 2048 * j, 0, t1A, 2048 * j, 0, gt1[j], 2048, 0, 2048, 152, 1, 0))],
              gathers=[(u1B, u1B[128 * j:128 * (j + 1), :], gu1[j]) for j in range(6)] + [(t1B, t1B[2048 * j:2048 * (j + 1), :], gt1[j]) for j in range(2)])
    if stop <= 7:
        return P
    mx1_phase(P, dict(di, u1F=u1F, t1F=t1F, y1F=y1F))
    if stop <= 8:
        return P
    if stop <= 8:
        return P
    if stop <= 9:
        return P
    tp_phase(P, dict(pre=dict(nfm=0, tmw=1024), ffn=(2, 6), final=True),
             dict(base, x_in=x4, x_out=out, modw=modw[1], modb=modb[1], ng=ng[1], w1=w1[1][1], w2=w2[1][1], wout=wout1, xs=xs, fg=fg, sel=sel,
                  gathers=[(y1F, y1F[1024 * j:1024 * (j + 1), :], gy1[j]) for j in range(8)],
                  tm_cand=lambda tok0: [(512 * r, 512, gy1[tok0 // 1024], gy1[tok0 // 1024][1024 * r + tok0 % 1024:1024 * r + tok0 % 1024 + 128, :],
                                         gy1[4 + tok0 // 1024], gy1[4 + tok0 // 1024][1024 * r + tok0 % 1024:1024 * r + tok0 % 1024 + 128, :]) for r in range(2)]))
    P.nc.sync.nop() if False else None
    return P


_FUSED = {}


def kernel(x, c, mod_w, mod_b, norm_g, ffn_w1, ffn_w2, mix0_in_w, lru_conv_w, lru_conv_b, lru_wa, lru_ba, lru_wx, lru_bx,
           lru_lambda, mix0_out_w, mix1_in_w, cmp_pos, cmp_w1, cmp_w2, mix1_out_w, final_norm_g):
    f32 = np.float32
    A = lambda a: np.ascontiguousarray(np.asarray(a, dtype=f32))
    x, c, mod_w, mod_b, norm_g, ffn_w1, ffn_w2 = map(A, (x, c, mod_w, mod_b, norm_g, ffn_w1, ffn_w2))
    mix0_in_w, mix0_out_w, mix1_in_w, mix1_out_w, final_norm_g = map(A, (mix0_in_w, mix0_out_w, mix1_in_w, mix1_out_w, final_norm_g))
    lru_conv_w, lru_conv_b, lru_wa, lru_ba, lru_wx, lru_bx, lru_lambda = map(A, (lru_conv_w, lru_conv_b, lru_wa, lru_ba, lru_wx, lru_bx, lru_lambda))
    cmp_pos, cmp_w1, cmp_w2 = map(A, (cmp_pos, cmp_w1, cmp_w2))
    if "P" not in _FUSED:
        _FUSED["P"] = build_fused()
    P = _FUSED["P"]
    consts = dict(mx0_consts())
    consts.update(mx1_consts())

    def half0(h):
        return np.concatenate([np.arange(256 * h, 256 * h + 256), 512 + np.arange(256 * h, 256 * h + 256), 1024 + np.arange(256 * h, 256 * h + 256),
                               1536 + np.arange(256 * h, 256 * h + 256), 2048 + np.arange(256 * h, 256 * h + 256)])

    def half1(g):
        r = lambda a, n: a + np.arange(n)
        return np.concatenate([r(512 * g, 512), r(1024 + 64 * g, 64), r(1152 + 64 * g, 64), r(1280 + 64 * g, 64), r(1536 + 64 * g, 64),
                               r(1408 + 64 * g, 64), r(1664 + 64 * g, 64), r(1792 + 24 * g, 24)])

    def bd(w, hh):
        m = np.zeros((256, 128), f32)
        for cc in range(2):
            for nl in range(2):
                n = hh * 4 + cc * 2 + nl
                m[cc * 128 + nl * 64:cc * 128 + nl * 64 + 64, nl * 64:nl * 64 + 64] = w[n]
        return m

    maps = []
    for core in range(8):
        b, p = core // 2, core % 2
        sl = slice(256 * p, 256 * p + 256)
        lrup = np.stack([lru_conv_w[0][0, sl], lru_conv_w[0][1, sl], lru_conv_w[0][2, sl], lru_conv_w[0][3, sl], lru_conv_b[0][sl],
                         lru_ba[0].reshape(-1)[sl], lru_bx[0].reshape(-1)[sl], lru_lambda[0][sl]], axis=1)
        selv = np.zeros((128, 2), f32)
        selv[:, p] = 1.0
        d = dict(x=np.ascontiguousarray(x[b, p * NT:(p + 1) * NT]), c=np.ascontiguousarray(c[b].reshape(8, 128).T), sel=selv,
                 modw0=mod_w[0], modw1=mod_w[1], modb0=mod_b[0:1], modb1=mod_b[1:2], ng0=norm_g[0], ng1=norm_g[1],
                 w1_00=ffn_w1[0, 0], w1_01=ffn_w1[0, 1], w1_10=ffn_w1[1, 0], w1_11=ffn_w1[1, 1],
                 w2_00=ffn_w2[0, 0], w2_01=ffn_w2[0, 1], w2_10=ffn_w2[1, 0], w2_11=ffn_w2[1, 1],
                 win0=np.ascontiguousarray(mix0_in_w[0][:, np.concatenate([half0(p), half0(1 - p)])]), wout0=mix0_out_w[0],
                 win1=np.ascontiguousarray(mix1_in_w[0][:, np.concatenate([half1(p), half1(1 - p)])]), wout1=mix1_out_w[0],
                 fg=final_norm_g.reshape(1, -1), lrup=np.ascontiguousarray(lrup.astype(f32)), wa=bd(lru_wa[0], p), wx=bd(lru_wx[0], p),
                 cpos=cmp_pos[0], cw1=cmp_w1[0], cw2=cmp_w2[0])
        d.update(consts)
        maps.append(d)
    res = run_bass_kernel_spmd(P.nc, maps, core_ids=list(range(8)))
    out = np.empty((4, 2 * NT, D), dtype=f32)
    for core in range(8):
        out[core // 2, (core % 2) * NT:(core % 2 + 1) * NT] = res.results[core]["out"]
    return out
```

```python
import numpy as np
import ml_dtypes
from contextlib import ExitStack
import concourse.bass as bass
import concourse.mybir as mybir
from concourse.bass_utils import run_bass_kernel_spmd

F32 = mybir.dt.float32
BF16 = mybir.dt.bfloat16
AF = mybir.ActivationFunctionType
ALU = mybir.AluOpType
AX = mybir.AxisListType


class T:
    def __init__(self, h, name):
        self.h = h
        self.name = name
        self.writers = {}
        self.readers = {}
        self.is_dram = False

    def __getitem__(self, k):
        return self.h[k]


class Op:
    __slots__ = ("stream", "agent", "fn", "deps", "isdma", "iscc")


class Prog:
    STREAMS = ("pe", "act", "dve", "pool", "sp")

    def __init__(self):
        self.nc = bass.Bass("TRN2", target_bir_lowering=False)
        self.es = ExitStack()
        self.tes = ExitStack()
        self.ops = []
        self.ntile = 0
        self.drams = []
        self.cnt = {}
        self.sems = {}
        self.free_dma_sems = []
        self.tot = dict(nops=0, nwait=0)

    def dram_in(self, name, shape, dt):
        return self._mkdram(self.nc.dram_tensor(name, list(shape), dt, kind="ExternalInput").ap(), name)

    def dram_out(self, name, shape, dt):
        return self._mkdram(self.nc.dram_tensor(name, list(shape), dt, kind="ExternalOutput").ap(), name)

    def dram_tmp(self, name, shape, dt):
        return self._mkdram(self.nc.dram_tensor(name, list(shape), dt, kind="Internal").ap(), name)

    def _mkdram(self, h, name):
        t = T(h, name)
        t.is_dram = True
        self.drams.append(t)
        return t

    def sb(self, name, shape, dt):
        self.ntile += 1
        h = self.tes.enter_context(self.nc.sbuf_tensor(f"{name}_{self.ntile}", list(shape), dt))
        return T(h, name)

    def ps(self, name, shape, dt=F32):
        self.ntile += 1
        h = self.tes.enter_context(self.nc.psum_tensor(f"{name}_{self.ntile}", list(shape), dt))
        return T(h, name)

    def op(self, stream, fn, r=(), w=(), wp=(), dma=False, dma_tile=None, cc=False):
        o = Op()
        o.stream = stream
        o.isdma = dma
        o.iscc = cc
        o.agent = ("q_%d" % id(dma_tile)) if dma else stream
        o.fn = fn
        idx = len(self.ops)
        deps = set()
        inorder = not dma
        for t in r:
            for a, j in t.writers.items():
                deps.add(j)
        for t in w:
            for a, j in t.writers.items():
                if not (inorder and a == o.agent):
                    deps.add(j)
            for a, j in t.readers.items():
                if not (inorder and a == o.agent):
                    deps.add(j)
        for t in wp:
            for a, j in t.readers.items():
                if not (inorder and a == o.agent):
                    deps.add(j)
        if stream == "pe" and not dma:
            deps = {j for j in deps if not (self.ops[j].agent == "pe")}
        o.deps = deps
        self.ops.append(o)
        for t in r:
            t.readers[o.agent] = idx
        for t in w:
            if t.readers:
                t.writers = {}
                t.readers = {}
            t.writers[o.agent] = idx
        for t in wp:
            if t.readers:
                t.writers = {}
                t.readers = {}
            t.writers[o.agent] = idx
        return idx

    def dma(self, out_t, out_ap, in_t, in_ap, q="sp", part=False, **kw):
        eng = self._eng(q)
        w = () if part else (out_t,)
        wp = (out_t,) if part else ()
        sbt = in_t if out_t.is_dram else out_t
        assert not sbt.is_dram
        self._keep = getattr(self, "_keep", [])
        self._keep.append(sbt)
        return self.op(q, lambda: eng.dma_start(out=out_ap, in_=in_ap, **kw), r=(in_t,), w=w, wp=wp, dma=True, dma_tile=sbt)

    def _eng(self, s):
        nc = self.nc
        return {"pe": nc.tensor, "act": nc.scalar, "dve": nc.vector, "pool": nc.gpsimd, "sp": nc.sync}[s]

    def allgather(self, in_t, out_t, groups, in_ap=None):
        nc = self.nc
        self._cck = getattr(self, "_cck", 0) + 1
        key = T(None, f"cc{self._cck}")
        self._keep = getattr(self, "_keep", [])
        self._keep.append(key)
        return self.op("pool", lambda: nc.gpsimd.collective_compute("AllGather", ALU.bypass, replica_groups=groups,
                                                                    ins=[in_t[:, :] if in_ap is None else in_ap], outs=[out_t[:, :]]),
                       r=(in_t,), w=(out_t,), dma=True, dma_tile=key, cc=True)

    def emit(self, final=True):
        nc = self.nc
        ops = self.ops
        sig = [False] * len(ops)
        for o in ops:
            for d in o.deps:
                sig[d] = True
        for i, o in enumerate(ops):
            if o.isdma:
                sig[i] = True
        cnt = self.cnt
        sems = self.sems
        val = [0] * len(ops)
        phase_agents = []
        for i, o in enumerate(ops):
            if sig[i]:
                if o.agent not in sems:
                    if o.isdma and not o.iscc and self.free_dma_sems:
                        s_, c_ = self.free_dma_sems.pop()
                        sems[o.agent] = s_
                        cnt[o.agent] = c_
                    else:
                        sems[o.agent] = self.es.enter_context(nc.semaphore(f"sem_{len(sems)}_{self.ntile}"))
                        cnt[o.agent] = 0
                if o.agent not in phase_agents:
                    phase_agents.append(o.agent)
                cnt[o.agent] += (1 if (o.iscc or not o.isdma) else 16)
                val[i] = cnt[o.agent]
        seen = {s: {} for s in self.STREAMS}
        nwait = 0
        for i, o in enumerate(ops):
            eng = self._eng(o.stream)
            need = {}
            for d in o.deps:
                a = ops[d].agent
                need[a] = max(need.get(a, 0), val[d])
            for a, v in need.items():
                if seen[o.stream].get(a, 0) < v:
                    eng.wait_ge(sems[a], v)
                    seen[o.stream][a] = v
                    nwait += 1
            ins = o.fn()
            if sig[i]:
                if o.iscc:
                    ins.then_inc(sems[o.agent])
                else:
                    ins.then_inc(sems[o.agent], 16 if o.isdma else 1)
        for s in self.STREAMS:
            eng = self._eng(s)
            for a in phase_agents:
                if seen[s].get(a, 0) < cnt[a]:
                    eng.wait_ge(sems[a], cnt[a])
        self.tot["nops"] += len(ops)
        self.tot["nwait"] += nwait
        self.stats = dict(self.tot)
        for a in phase_agents:
            if a.startswith("q_") and not any(o.iscc and o.agent == a for o in ops):
                self.free_dma_sems.append((sems.pop(a), cnt.pop(a)))
        self.ops = []
        for t in self.drams:
            t.writers = {}
            t.readers = {}
        self.tes.close()
        self.tes = ExitStack()
        return nc


D = 1024
DFF = 2816
NJ = DFF // 128
NT = 4096
TG = 256
EPS = 1e-6


def tp_phase(P, cfg, io):
    nc = P.nc
    pre, post, final = cfg.get("pre"), cfg.get("post"), cfg.get("final", False)
    ffn_norm, ffn_mod = cfg["ffn"]
    x_in, c_in, modw, modb, ng, w1, w2, identd, x_out = (io[k] for k in ("x_in", "c", "modw", "modb", "ng", "w1", "w2", "ident", "x_out"))
    if pre:
        nfm, tmw = pre["nfm"], pre["tmw"]
        wout = io["wout"]
        xs_d = io["xs"]
    if post:
        wc = post["wc"]
        win = io["win"]
    if final:
        fgd = io["fg"]

    wbig = P.sb("wbig", [128, 8 * 2 * DFF], BF16)
    w2b = P.sb("w2b", [128, NJ, D], BF16)
    stage = [P.sb(f"stage{i}", [128, 1024], F32) for i in range(2)]
    ident_f = P.sb("identf", [128, 128], F32)
    ident = P.sb("ident", [128, 128], BF16)
    ones_f = P.sb("ones", [128, 128], F32)
    cT = P.sb("cT", [128, 8], F32)
    cbc = P.sb("cbc", [128, 8, 128], F32)
    rowA = P.sb("rowA", [128, D], F32)
    rowB = P.sb("rowB", [128, D], F32)
    rowsm = P.sb("rowsm", [1, D], F32)
    xbuf = [P.sb(f"xb{i}", [128, D], F32) for i in range(2)]
    xrb = [P.sb(f"xr{i}", [128, D], F32) for i in range(2)]
    hb = [P.sb(f"hb{i}", [128, D], BF16) for i in range(2)]
    junk = P.sb("junk", [128, D], BF16)
    sm = [P.sb(f"sm{i}", [128, 4], F32) for i in range(2)]
    mhalf = P.sb("mhalf", [128, 1], F32)
    hT = [P.sb(f"hT{i}", [128, 8, TG], BF16) for i in range(2)]
    actT = P.sb("actT", [128, NJ, TG], BF16)
    sa = [P.sb(f"sa{i}", [128, TG], F32) for i in range(2)]
    if final:
        rowF = P.sb("rowF", [128, D], F32)
    pa = [P.ps(f"pa{i}", [128, 512]) for i in range(2)]
    pb = [P.ps(f"pb{i}", [128, 512]) for i in range(2)]
    po = [P.ps(f"po{i}", [128, 512]) for i in range(2)]
    ptr = [P.ps(f"ptr{i}", [128, 1024], BF16) for i in range(2)]

    cnt = {"bl": 0, "st": 0, "x": 0, "po": 0, "pa": 0, "ptr": 0, "hb": 0, "sa": 0, "xr": 0, "xo": 0, "sm": 0}

    def nxt(k, n=2):
        v = cnt[k] % n
        cnt[k] += 1
        return v

    P.dma(ident_f, ident_f[:, :], identd, identd[:, :])
    P.op("pool", lambda: nc.gpsimd.tensor_copy(out=ident[:, :], in_=ident_f[:, :]), r=[ident_f], w=[ident])
    P.op("pool", lambda: nc.gpsimd.memset(ones_f[:, :], 1.0), w=[ones_f])
    P.op("pool", lambda: nc.gpsimd.memset(mhalf[:, :], -0.5), w=[mhalf])
    P.dma(cT, cT[:, :], c_in, c_in[:, :])
    P.op("act", lambda: nc.scalar.activation(out=cT[:, :], in_=cT[:, :], func=AF.Silu), r=[cT], w=[cT])
    for kc in range(8):
        P.op("dve", lambda kc=kc: nc.vector.tensor_scalar(out=cbc[:, kc, :], in0=ones_f[:, :], scalar1=cT[:, kc:kc + 1],
                                                         scalar2=None, op0=ALU.mult), r=[ones_f, cT], wp=[cbc])

    def mod_piece(idx, dst, plus1=False):
        P.dma(rowsm, rowsm[0:1, :], modb, modb[0:1, idx * D:(idx + 1) * D])
        pp = [po[0], po[1]]
        for kc in range(8):
            st = stage[nxt("st")]
            P.dma(st, st[:, 0:D], modw, modw[kc * 128:(kc + 1) * 128, idx * D:(idx + 1) * D])
            for h in range(2):
                P.op("pe", lambda kc=kc, h=h, st=st: nc.tensor.matmul(pp[h][:, :], lhsT=cbc[:, kc, :], rhs=st[:, h * 512:(h + 1) * 512],
                                                                     start=(kc == 0), stop=False),
                     r=[cbc, st], w=[pp[h]] if kc == 0 else (), wp=[pp[h]] if kc else ())
        for h in range(2):
            P.op("pe", lambda h=h: nc.tensor.matmul(pp[h][:, :], lhsT=ones_f[0:1, :], rhs=rowsm[0:1, h * 512:(h + 1) * 512],
                                                    start=False, stop=True), r=[ones_f, rowsm], wp=[pp[h]])
            if plus1:
                P.op("dve", lambda h=h: nc.vector.tensor_scalar_add(out=dst[:, h * 512:(h + 1) * 512], in0=pp[h][:, :], scalar1=1.0),
                     r=[pp[h]], wp=[dst])
            else:
                P.op("dve", lambda h=h: nc.vector.tensor_copy(out=dst[:, h * 512:(h + 1) * 512], in_=pp[h][:, :]), r=[pp[h]], wp=[dst])

    def row_bc(src_t, src_ap, dst, mul_into=False):
        rowsm2 = rowsm
        P.dma(rowsm2, rowsm2[0:1, :], src_t, src_ap)
        for h in range(2):
            pp = po[h]
            P.op("pe", lambda h=h, pp=pp: nc.tensor.matmul(pp[:, :], lhsT=ones_f[0:1, :], rhs=rowsm2[0:1, h * 512:(h + 1) * 512],
                                                           start=True, stop=True), r=[ones_f, rowsm2], w=[pp])
            if mul_into:
                P.op("dve", lambda h=h, pp=pp: nc.vector.tensor_tensor(out=dst[:, h * 512:(h + 1) * 512], in0=dst[:, h * 512:(h + 1) * 512],
                                                                       in1=pp[:, :], op=ALU.mult), r=[pp, dst], wp=[dst])
            else:
                P.op("dve", lambda h=h, pp=pp: nc.vector.tensor_copy(out=dst[:, h * 512:(h + 1) * 512], in_=pp[:, :]), r=[pp], wp=[dst])

    cast_engs = ["pool", "dve", "act"]

    def cast(eng, out_ap, in_ap, r, wp, mul_ap=None, mul_t=None):
        if mul_ap is not None:
            e = "pool" if eng == "act" else eng
            ee = nc.gpsimd if e == "pool" else nc.vector
            P.op(e, lambda: ee.tensor_tensor(out=out_ap, in0=in_ap, in1=mul_ap, op=ALU.mult), r=list(r) + [mul_t], wp=wp)
        elif eng == "act":
            P.op("act", lambda: nc.scalar.copy(out=out_ap, in_=in_ap), r=r, wp=wp)
        else:
            ee = nc.gpsimd if eng == "pool" else nc.vector
            P.op(eng, lambda: ee.tensor_copy(out=out_ap, in_=in_ap), r=r, wp=wp)

    def load_w(dst, dst_view, src_t, rows_kc, ncols, mul_t=None):
        k = 0
        for kc in range(rows_kc):
            for c0 in range(0, ncols, 1024):
                cw = min(1024, ncols - c0)
                st = stage[nxt("st")]
                P.dma(st, st[:, 0:cw], src_t, src_t[kc * 128:(kc + 1) * 128, c0:c0 + cw])
                cast(cast_engs[k % 3], dst_view(kc)[:, c0:c0 + cw], st[:, 0:cw], [st], [dst],
                     mul_ap=(mul_t[:, c0:c0 + cw] if mul_t is not None else None), mul_t=mul_t)
                k += 1

    W1C = 2 * DFF

    def w1v(kc):
        return wbig[:, kc * W1C:(kc + 1) * W1C]

    mod_piece(ffn_mod + 1, rowA, plus1=True)
    row_bc(ng, ng[ffn_norm:ffn_norm + 1, :], rowA, mul_into=True)
    mod_piece(ffn_mod + 0, rowB)
    rtmp = xbuf[0]
    mod_piece(ffn_mod + 2, rtmp)
    P.op("dve", lambda: nc.vector.tensor_scalar_mul(out=rtmp[:, :], in0=rtmp[:, :], scalar1=0.5), r=[rtmp], w=[rtmp])
    load_w(w2b, lambda j: w2b[:, j, :], w2, NJ, D, mul_t=rtmp)
    xsrc = x_in
    if pre:
        xsrc = xs_d
        for (gi_t, gi_ap, go_t) in io.get("gathers", []):
            P.allgather(gi_t, go_t, PAIRS, in_ap=gi_ap)
        selt = P.sb("selt", [128, 2], F32)
        P.dma(selt, selt[:, :], io["sel"], io["sel"][:, :])

        def blend_load(ncols, parts):
            sa_, sb_ = ((stage[0], stage[1]), (xbuf[0], xbuf[1]))[nxt("bl")]
            for (c0, wd, A_t, A_ap, B_t, B_ap, vw) in parts:
                P.dma(sa_, vw(sa_, c0, wd), A_t, A_ap, part=True)
                P.dma(sb_, vw(sb_, c0, wd), B_t, B_ap, part=True)
            P.op("act", lambda: nc.scalar.activation(out=sa_[:, 0:ncols], in_=sa_[:, 0:ncols], func=AF.Copy, scale=selt[:, 0:1]), r=[sa_, selt], w=[sa_])
            P.op("dve", lambda: nc.vector.scalar_tensor_tensor(out=sa_[:, 0:ncols], in0=sb_[:, 0:ncols], scalar=selt[:, 1:2], in1=sa_[:, 0:ncols],
                                                               op0=ALU.mult, op1=ALU.add), r=[sa_, sb_, selt], w=[sa_])
            return sa_

        mod_piece(5, rtmp)
        load_w(wbig, lambda kc: wbig[:, kc * D:(kc + 1) * D], wout, 8, D, mul_t=rtmp)
        ntm = tmw // 128
        for t in range(NT // 128):
            tok0 = t * 128
            yTt = hT[t % 2]
            xt = xrb[nxt("xr")]
            P.dma(xt, xt[:, :], x_in, x_in[tok0:tok0 + 128, :])
            if nfm:
                A_t, A_ap, B_t, B_ap = io["fm_cand"](tok0)
                st = blend_load(nfm * 128, [(0, nfm * 128, A_t, A_ap.rearrange("(k p) c -> p k c", p=128), B_t, B_ap.rearrange("(k p) c -> p k c", p=128),
                                            lambda tl, c0, wd: tl[:, c0:c0 + wd].rearrange("p (k c) -> p k c", k=nfm))])
                P.op("pool", lambda st=st, yTt=yTt: nc.gpsimd.tensor_copy(out=yTt[:, 0:nfm, 0:128],
                                                                      in_=st[:, 0:nfm * 128].rearrange("p (k c) -> p k c", k=nfm)),
                     r=[st], wp=[yTt])
            st2 = blend_load(tmw, [(c0, wd, A_t, A_ap, B_t, B_ap, lambda tl, c0, wd: tl[:, c0:c0 + wd])
                                   for (c0, wd, A_t, A_ap, B_t, B_ap) in io["tm_cand"](tok0)])
            h = hb[nxt("hb")]
            P.op("pool", lambda st2=st2, h=h: nc.gpsimd.tensor_copy(out=h[:, 0:tmw], in_=st2[:, 0:tmw]), r=[st2], w=[h])
            pt = ptr[nxt("ptr")]
            for kc in range(ntm):
                P.op("pe", lambda kc=kc, pt=pt, h=h: nc.tensor.transpose(pt[:, kc * 128:(kc + 1) * 128], h[:, kc * 128:(kc + 1) * 128], ident[:, :]),
                     r=[h, ident], w=[pt] if kc == 0 else (), wp=[pt] if kc else ())
            P.op("act", lambda pt=pt, yTt=yTt: nc.scalar.copy(out=yTt[:, nfm:nfm + ntm, 0:128],
                                                          in_=pt[:, 0:ntm * 128].rearrange("p (k c) -> p k c", k=ntm)),
                 r=[pt], wp=[yTt])
            for h2 in range(2):
                pp = po[nxt("po")]
                for kc in range(8):
                    P.op("pe", lambda kc=kc, pp=pp, h2=h2, yTt=yTt: nc.tensor.matmul(pp[:, :], lhsT=yTt[:, kc, 0:128],
                                                                                   rhs=wbig[:, kc * D + h2 * 512:kc * D + (h2 + 1) * 512],
                                                                                   start=(kc == 0), stop=(kc == 7)),
                         r=[yTt, wbig], w=[pp] if kc == 0 else (), wp=[pp] if kc else ())
                P.op("dve", lambda pp=pp, xt=xt, h2=h2: nc.vector.tensor_tensor(out=xt[:, h2 * 512:(h2 + 1) * 512], in0=pp[:, :],
                                                                              in1=xt[:, h2 * 512:(h2 + 1) * 512], op=ALU.add),
                     r=[pp, xt], w=[xt])
            P.dma(xs_d, xs_d[tok0:tok0 + 128, :], xt, xt[:, :], q="pool", part=True)
    load_w(wbig, w1v, w1, 8, W1C)
    if final:
        row_bc(fgd, fgd[0:1, :], rowF)

    def rms_rstd(xt_t, xt_ap):
        s = sm[nxt("sm")]
        P.op("dve", lambda: nc.vector.scalar_tensor_tensor(out=junk[:, :], in0=xt_ap, scalar=1.0, in1=xt_ap, op0=ALU.mult, op1=ALU.mult,
                                                           accum_out=s[:, 0:1]), r=[xt_t], w=[junk, s])
        P.op("pool", lambda: nc.gpsimd.tensor_scalar(out=s[:, 1:2], in0=s[:, 0:1], scalar1=1.0 / D, scalar2=EPS, op0=ALU.mult, op1=ALU.add),
             r=[s], w=[s])
        P.op("pool", lambda: nc.gpsimd.tensor_tensor(out=s[:, 2:3], in0=s[:, 1:2], in1=mhalf[:, 0:1], op=ALU.pow), r=[s, mhalf], w=[s])
        return s

    def norm_mod(xt_t, A, B):
        s = rms_rstd(xt_t, xt_t[:, :])
        h = hb[nxt("hb")]
        P.op("dve", lambda: nc.vector.scalar_tensor_tensor(out=xt_t[:, :], in0=xt_t[:, :], scalar=s[:, 2:3], in1=A[:, :], op0=ALU.mult, op1=ALU.mult),
             r=[xt_t, s, A], w=[xt_t])
        P.op("dve", lambda: nc.vector.tensor_tensor(out=h[:, :], in0=xt_t[:, :], in1=B[:, :], op=ALU.add), r=[xt_t, B], w=[h])
        return h

    def transpose_into(h, hTt, col0, nk=8):
        pt = ptr[nxt("ptr")]
        for kc in range(nk):
            P.op("pe", lambda kc=kc: nc.tensor.transpose(pt[:, kc * 128:(kc + 1) * 128], h[:, kc * 128:(kc + 1) * 128], ident[:, :]),
                 r=[h, ident], w=[pt] if kc == 0 else (), wp=[pt] if kc else ())
        P.op("act", lambda: nc.scalar.copy(out=hTt[:, 0:nk, col0:col0 + 128], in_=pt[:, 0:nk * 128].rearrange("p (k c) -> p k c", k=nk)),
             r=[pt], wp=[hTt])

    NG = NT // TG
    TPG = TG // 128

    hpend = {}

    def prep(g):
        hTt = hT[g % 2]
        hs = []
        for i in range(TPG):
            tok0 = g * TG + i * 128
            xt = xbuf[nxt("x")]
            P.dma(xt, xt[:, :], xsrc, xsrc[tok0:tok0 + 128, :])
            hs.append(norm_mod(xt, rowA, rowB))
        hpend[g] = hs

    def trans(g):
        for i, h in enumerate(hpend.pop(g)):
            transpose_into(h, hT[g % 2], i * 128)

    prep(0)
    trans(0)
    for g in range(NG):
        hTt = hT[g % 2]
        for j in range(NJ):
            if j == 8 and g + 1 < NG:
                prep(g + 1)
            k = nxt("pa")
            ppa, ppb = pa[k], pb[k]
            for (pp, col) in ((ppa, j * 128), (ppb, DFF + j * 128)):
                for kc in range(8):
                    P.op("pe", lambda kc=kc, pp=pp, col=col, hTt=hTt: nc.tensor.matmul(pp[:, 0:TG], lhsT=wbig[:, kc * W1C + col:kc * W1C + col + 128],
                                                                                     rhs=hTt[:, kc, :], start=(kc == 0), stop=(kc == 7)),
                         r=[wbig, hTt], w=[pp] if kc == 0 else (), wp=[pp] if kc else ())
            s_ = sa[nxt("sa")]
            P.op("act", lambda s_=s_, ppa=ppa: nc.scalar.activation(out=s_[:, :], in_=ppa[:, 0:TG], func=AF.Silu), r=[ppa], w=[s_])
            P.op("dve", lambda s_=s_, ppb=ppb, j=j: nc.vector.tensor_tensor(out=actT[:, j, :], in0=s_[:, :], in1=ppb[:, 0:TG], op=ALU.mult),
                 r=[s_, ppb], wp=[actT])
        if g + 1 < NG:
            trans(g + 1)
        for i in range(TPG):
            tok0 = g * TG + i * 128
            xr = xrb[nxt("xr")]
            P.dma(xr, xr[:, :], xsrc, xsrc[tok0:tok0 + 128, :])
            xo = xr
            for h2 in range(2):
                pp = po[nxt("po")]
                for j in range(NJ):
                    P.op("pe", lambda j=j, pp=pp, h2=h2, i=i: nc.tensor.matmul(pp[:, :], lhsT=actT[:, j, i * 128:(i + 1) * 128],
                                                                             rhs=w2b[:, j, h2 * 512:(h2 + 1) * 512], start=(j == 0), stop=(j == NJ - 1)),
                         r=[actT, w2b], w=[pp] if j == 0 else (), wp=[pp] if j else ())
                P.op("dve", lambda pp=pp, xr=xr, xo=xo, h2=h2: nc.vector.tensor_tensor(out=xo[:, h2 * 512:(h2 + 1) * 512], in0=pp[:, :],
                                                                                     in1=xr[:, h2 * 512:(h2 + 1) * 512], op=ALU.add),
                     r=[pp, xr], w=[xo])
            if final:
                s = rms_rstd(xo, xo[:, :])
                P.op("dve", lambda xo=xo, s=s: nc.vector.scalar_tensor_tensor(out=xo[:, :], in0=xo[:, :], scalar=s[:, 2:3], in1=rowF[:, :],
                                                                            op0=ALU.mult, op1=ALU.mult), r=[xo, s, rowF], w=[xo])
            P.dma(x_out, x_out[tok0:tok0 + 128, :], xo, xo[:, :], q="pool", part=True)

    if post:
        pn, pm = post["norm_idx"], post["mod_idx0"]
        mod_piece(pm + 1, rowA, plus1=True)
        row_bc(ng, ng[pn:pn + 1, :], rowA, mul_into=True)
        mod_piece(pm + 0, rowB)
        load_w(wbig, lambda kc: wbig[:, kc * wc:(kc + 1) * wc], win, 8, wc)
        uo = [xrb[0], xrb[1], sa[0], sa[1]]
        cnt["uo"] = 0
        ppend = {}

        def prepP(g):
            hs = []
            for i in range(TPG):
                tok0 = g * TG + i * 128
                xt = xbuf[nxt("x")]
                P.dma(xt, xt[:, :], x_out, x_out[tok0:tok0 + 128, :])
                hs.append(norm_mod(xt, rowA, rowB))
            ppend[g] = hs

        def transP(g):
            for i, h in enumerate(ppend.pop(g)):
                transpose_into(h, hT[g % 2], i * 128)

        prepP(0)
        transP(0)
        for g in range(NG):
            hTt = hT[g % 2]
            if g + 1 < NG:
                prepP(g + 1)
            for m, (col, dstT, row0) in enumerate(post["fm"]):
                if m == len(post["fm"]) // 2 and g + 1 < NG:
                    transP(g + 1)
                pp = pa[nxt("pa")]
                for kc in range(8):
                    P.op("pe", lambda kc=kc, pp=pp, col=col, hTt=hTt: nc.tensor.matmul(pp[:, 0:TG], lhsT=wbig[:, kc * wc + col:kc * wc + col + 128],
                                                                                     rhs=hTt[:, kc, :], start=(kc == 0), stop=(kc == 7)),
                         r=[wbig, hTt], w=[pp] if kc == 0 else (), wp=[pp] if kc else ())
                u = uo[nxt("uo", 4)]
                if m % 2 == 0:
                    P.op("act", lambda u=u, pp=pp: nc.scalar.copy(out=u[:, 0:TG], in_=pp[:, 0:TG]), r=[pp], w=[u])
                else:
                    P.op("dve", lambda u=u, pp=pp: nc.vector.tensor_copy(out=u[:, 0:TG], in_=pp[:, 0:TG]), r=[pp], w=[u])
                P.dma(dstT, dstT[row0:row0 + 128, g * TG:(g + 1) * TG], u, u[:, 0:TG], q="pool", part=True)
            for i in range(TPG):
                tok0 = g * TG + i * 128
                for (col, wd, dstT, oc) in post["tm"]:
                    pp = po[nxt("po")]
                    for kc in range(8):
                        P.op("pe", lambda kc=kc, pp=pp, col=col, wd=wd, hTt=hTt, i=i: nc.tensor.matmul(pp[:, 0:wd], lhsT=hTt[:, kc, i * 128:(i + 1) * 128],
                                                                                                   rhs=wbig[:, kc * wc + col:kc * wc + col + wd],
                                                                                                   start=(kc == 0), stop=(kc == 7)),
                             r=[wbig, hTt], w=[pp] if kc == 0 else (), wp=[pp] if kc else ())
                    u = uo[nxt("uo", 4)]
                    P.op("dve", lambda u=u, pp=pp, wd=wd: nc.vector.tensor_copy(out=u[:, 0:wd], in_=pp[:, 0:wd]), r=[pp], w=[u])
                    P.dma(dstT, dstT[tok0:tok0 + 128, oc:oc + wd], u, u[:, 0:wd], q="pool", part=True)
    P.emit()


S = 8192
TC = 2048
NEGB = -30000.0


def mx0_phase(P, io):
    nc = P.nc

    def V(eng, name, r, w=(), wp=(), **kw):
        e = P._eng(eng)
        P.op(eng, lambda: getattr(e, name)(**kw), r=r, w=w, wp=wp)

    uF, vd, lrup, wad, wxd, ropeC, ropeS, onehot, cmd, pastd, ownd, identd, yl, ya = (io[k] for k in (
        "uF", "vF", "lrup", "wa", "wx", "ropeC", "ropeS", "onehot", "cm", "past", "own", "ident", "ylF", "yaF"))

    ident_f = P.sb("identf", [128, 128], F32)
    cm = P.sb("cm", [128, 4 * 512], BF16)
    past = P.sb("past", [128, 32 * 32], F32)
    own = P.sb("own", [128, 32 * 32], F32)
    P.dma(ident_f, ident_f[:, :], identd, identd[:, :])
    P.dma(cm, cm[:, :], cmd, cmd[:, :])
    P.dma(past, past[:, :], pastd, pastd[:, :])
    P.dma(own, own[:, :], ownd, ownd[:, :])

    lp = P.sb("lp", [128, 2, 8], F32)
    wa = P.sb("wa", [128, 2, 128], F32)
    wx = P.sb("wx", [128, 2, 128], F32)
    c1 = P.sb("c1", [128, 2], F32)
    P.dma(lp, lp[:, :, :], lrup, lrup[:, :].rearrange("(c p) k -> p c k", p=128))
    P.dma(wa, wa[:, :, :], wad, wad[:, :].rearrange("(c p) k -> p c k", p=128))
    P.dma(wx, wx[:, :, :], wxd, wxd[:, :].rearrange("(c p) k -> p c k", p=128))
    V("act", "activation", [lp], w=[c1], out=c1[:, :], in_=lp[:, :, 7], func=AF.Exp, scale=-1.0)
    V("act", "activation", [c1], w=[c1], out=c1[:, :], in_=c1[:, :], func=AF.Ln, bias=1.0, scale=1.0)
    V("dve", "tensor_scalar_mul", [c1], w=[c1], out=c1[:, :], in0=c1[:, :], scalar1=-8.0)

    xp = [P.sb(f"xp{i}", [128, TC + 3], F32) for i in range(2)]
    xc = P.sb("xc", [128, TC], F32)
    rr = P.sb("rr", [128, TC], F32)
    ii = P.sb("ii", [128, TC], F32)
    aa = P.sb("aa", [128, TC], F32)
    t1 = P.sb("t1", [128, TC], F32)
    hh = [P.sb(f"hh{i}", [128, TC], F32) for i in range(2)]
    gg = P.sb("gg", [128, TC], F32)
    g2 = P.sb("g2", [128, TC], F32)
    pl = [P.ps(f"pl{i}", [128, 512]) for i in range(2)]
    nps = 0

    def lru_step(c, tcn):
        nonlocal nps
        if True:
            k = (c * (S // TC) + tcn)
            xpt = xp[k % 2]
            xpn = xp[(k + 1) % 2]
            t0 = tcn * TC
            if tcn == 0:
                V("pool", "memset", [], wp=[xpt], ap=xpt[:, 0:3], constant=0.0)
            P.dma(xpt, xpt[:, 3:3 + TC], uF, uF[c * 128:(c + 1) * 128, t0:t0 + TC], part=True)
            P.dma(gg, gg[:, :], uF, uF[256 + c * 128:256 + (c + 1) * 128, t0:t0 + TC])
            V("dve", "tensor_scalar", [xpt, lp], w=[xc], out=xc[:, :], in0=xpt[:, 3:3 + TC], scalar1=lp[:, c, 3:4], scalar2=lp[:, c, 4:5],
              op0=ALU.mult, op1=ALU.add)
            for kk in range(3):
                V("dve", "scalar_tensor_tensor", [xpt, lp, xc], w=[xc], out=xc[:, :], in0=xpt[:, kk:kk + TC], scalar=lp[:, c, kk:kk + 1], in1=xc[:, :],
                  op0=ALU.mult, op1=ALU.add)
            if tcn + 1 < S // TC:
                V("pool", "tensor_copy", [xpt], wp=[xpn], out=xpn[:, 0:3], in_=xpt[:, TC:TC + 3])
            for n in range(TC // 512):
                for (wm, dst, bcol) in ((wa, rr, 5), (wx, ii, 6)):
                    pp = pl[nps % 2]
                    nps += 1
                    P.op("pe", lambda pp=pp, wm=wm, n=n, c=c: nc.tensor.matmul(pp[:, :], lhsT=wm[:, c, :], rhs=xc[:, n * 512:(n + 1) * 512], start=True, stop=True),
                         r=[wm, xc], w=[pp])
                    V("act", "activation", [pp, lp], wp=[dst], out=dst[:, n * 512:(n + 1) * 512], in_=pp[:, :], func=AF.Sigmoid, bias=lp[:, c, bcol:bcol + 1], scale=1.0)
            V("act", "activation", [rr, c1], w=[aa], out=aa[:, :], in_=rr[:, :], func=AF.Exp, scale=c1[:, c:c + 1])
            V("pool", "tensor_tensor", [aa], w=[t1], out=t1[:, :], in0=aa[:, :], in1=aa[:, :], op=ALU.mult)
            V("act", "activation", [t1], w=[t1], out=t1[:, :], in_=t1[:, :], func=AF.Sqrt, bias=1.0, scale=-1.0)
            V("pool", "tensor_tensor", [ii, xc], w=[ii], out=ii[:, :], in0=ii[:, :], in1=xc[:, :], op=ALU.mult)
            V("pool", "tensor_tensor", [ii, t1], w=[ii], out=ii[:, :], in0=ii[:, :], in1=t1[:, :], op=ALU.mult)
            ht = hh[k % 2]
            hprev = hh[(k + 1) % 2]
            if tcn == 0:
                V("dve", "tensor_tensor_scan", [aa, ii], w=[ht], out=ht[:, :], data0=aa[:, :], data1=ii[:, :], initial=0.0, op0=ALU.mult, op1=ALU.add)
            else:
                V("dve", "tensor_tensor_scan", [aa, ii, hprev], w=[ht], out=ht[:, :], data0=aa[:, :], data1=ii[:, :], initial=hprev[:, TC - 1:TC],
                  op0=ALU.mult, op1=ALU.add)
            V("pool", "tensor_tensor", [gg], w=[g2], out=g2[:, :], in0=gg[:, :], in1=gg[:, :], op=ALU.mult)
            V("pool", "tensor_scalar", [g2], w=[g2], out=g2[:, :], in0=g2[:, :], scalar1=0.044715, scalar2=1.0, op0=ALU.mult, op1=ALU.add)
            V("pool", "tensor_tensor", [g2, gg], w=[g2], out=g2[:, :], in0=g2[:, :], in1=gg[:, :], op=ALU.mult)
            V("act", "activation", [g2], w=[g2], out=g2[:, :], in_=g2[:, :], func=AF.Sigmoid, scale=1.5957691216057308)
            V("pool", "tensor_tensor", [g2, gg], w=[g2], out=g2[:, :], in0=g2[:, :], in1=gg[:, :], op=ALU.mult)
            V("dve", "tensor_tensor", [g2, ht], w=[g2], out=g2[:, :], in0=g2[:, :], in1=ht[:, :], op=ALU.mult)
            P.dma(yl[tcn], yl[tcn][c * 128:(c + 1) * 128, :], g2, g2[:, :], q="pool", part=True)

    lru_steps = [(c, tcn) for c in range(2) for tcn in range(S // TC)]

    kTa = P.sb("kTa", [96, S], BF16)
    qTa = P.sb("qTa", [96, S], BF16)
    Va = P.sb("Va", [128, 64, 65], BF16)
    st = [P.sb(f"st{i}", [64, TC], F32) for i in range(2)]
    sw = P.sb("sw", [16, TC], F32)
    Cc = P.sb("Cc", [16, TC], F32)
    Sc = P.sb("Sc", [16, TC], F32)
    cenT = P.sb("cenT", [64, 32], F32)
    gm = P.sb("gm", [128, 32], F32)
    m8 = P.sb("m8", [128, 8], F32)
    bf = P.sb("bf", [128, 32], F32)
    vst = [P.sb(f"vst{i}", [128, 16, 64], F32) for i in range(2)]
    ptile = [P.sb(f"pt{i}", [128, 512], BF16) for i in range(3)]
    osb = P.sb("osb", [128, 4, 65], F32)
    rden = P.sb("rden", [128, 4, 1], F32)
    yt = [P.sb(f"yt{i}", [128, 4, 64], F32) for i in range(2)]
    ps = [P.ps(f"ps{i}", [128, 512]) for i in range(3)]
    po = [P.ps(f"po{i}", [128, 512]) for i in range(2)]
    pg = P.ps("pg", [128, 512])
    nst = 0
    V("pool", "memset", [], w=[Va], ap=Va[:, :, 64:65], constant=1.0)
    nq = 0
    for h in range(4):
        P.dma(kTa, kTa[64:96, :], onehot, onehot[:, :], part=True)
        for vc in range(4):
            vs_ = vst[vc % 2]
            P.dma(vs_, vs_[:, :, :], vd, vd[vc * 2048:(vc + 1) * 2048, h * 64:(h + 1) * 64].rearrange("(t p) d -> p t d", p=128))
            V("pool", "tensor_copy", [vs_], wp=[Va], out=Va[:, vc * 16:(vc + 1) * 16, 0:64], in_=vs_[:, :, :])

        def rope_chunk(src, tcn, dstA):
            nonlocal nst
            s_ = st[nst % 2]
            nst += 1
            t0 = tcn * TC
            P.dma(s_, s_[:, :], uF, uF[src + h * 64:src + (h + 1) * 64, t0:t0 + TC])
            P.dma(sw, sw[0:8, :], uF, uF[src + h * 64 + 8:src + h * 64 + 16, t0:t0 + TC], part=True)
            P.dma(sw, sw[8:16, :], uF, uF[src + h * 64:src + h * 64 + 8, t0:t0 + TC], part=True)
            P.dma(Cc, Cc[:, :], ropeC, ropeC[:, t0:t0 + TC])
            P.dma(Sc, Sc[:, :], ropeS, ropeS[:, t0:t0 + TC])
            V("pool", "tensor_tensor", [sw, Sc], w=[sw], out=sw[:, :], in0=sw[:, :], in1=Sc[:, :], op=ALU.mult)
            V("dve", "tensor_tensor", [s_, Cc], w=[s_], out=s_[0:16, :], in0=s_[0:16, :], in1=Cc[:, :], op=ALU.mult)
            V("dve", "tensor_tensor", [s_, sw], w=[s_], out=s_[0:16, :], in0=s_[0:16, :], in1=sw[:, :], op=ALU.add)
            V("act", "copy", [s_], wp=[dstA], out=dstA[0:64, t0:t0 + TC], in_=s_[:, :])
            return s_

        for tcn in range(S // TC):
            s_ = rope_chunk(768, tcn, kTa)
            V("dve", "tensor_reduce", [s_], wp=[cenT], out=cenT[:, tcn * 8:(tcn + 1) * 8], in_=s_[:, :].rearrange("p (n k) -> p n k", k=256),
              axis=AX.X, op=ALU.add)
        for tcn in range(S // TC):
            s_ = rope_chunk(512, tcn, qTa)
            for qi in range(TC // 128):
                qt = tcn * (TC // 128) + qi
                qb = qt // 2
                P.op("pe", lambda s_=s_, qi=qi: nc.tensor.matmul(pg[:, 0:32], lhsT=s_[:, qi * 128:(qi + 1) * 128], rhs=cenT[:, :], start=True, stop=True),
                     r=[s_, cenT], w=[pg])
                V("dve", "tensor_tensor", [pg, past], w=[gm], out=gm[:, :], in0=pg[:, 0:32], in1=past[:, qb * 32:(qb + 1) * 32], op=ALU.add)
                V("dve", "max", [gm], w=[m8], out=m8[:, :], in_=gm[:, :])
                V("dve", "tensor_scalar_max", [m8], w=[m8], out=m8[:, 2:3], in0=m8[:, 2:3], scalar1=-1e29)
                V("dve", "scalar_tensor_tensor", [gm, m8, own], w=[bf], out=bf[:, :], in0=gm[:, :], scalar=m8[:, 2:3], in1=own[:, qb * 32:(qb + 1) * 32],
                  op0=ALU.is_ge, op1=ALU.add)
                V("dve", "tensor_scalar", [bf], w=[bf], out=bf[:, :], in0=bf[:, :], scalar1=-NEGB, scalar2=NEGB, op0=ALU.mult, op1=ALU.add)
                P.op("pe", lambda: nc.tensor.transpose(pg[0:32, 128:256], bf[:, :], ident_f[:, :]), r=[bf, ident_f], w=[pg])
                V("act", "copy", [pg], wp=[qTa], out=qTa[64:96, qt * 128:(qt + 1) * 128], in_=pg[0:32, 128:256])
        inflight = []

        def issue0(g, kt, h=h):
            nonlocal nq
            pp = ps[nq % 3]
            pt = ptile[nq % 3]
            nq += 1
            P.op("pe", lambda: nc.tensor.matmul(pp[:, :], lhsT=kTa[:, kt * 128:(kt + 1) * 128], rhs=qTa[:, g * 512:(g + 1) * 512],
                                                start=True, stop=True), r=[kTa, qTa], w=[pp])
            return (g, kt, pp, pt)

        def finish0(g, kt, pp, pt, h=h):
            pacc = po[g % 2]
            V("act", "activation", [pp], w=[pt], out=pt[:, :], in_=pp[:, :], func=AF.Exp, scale=0.125)
            j = kt - 4 * g
            if j >= 0:
                V("pool" if kt % 2 else "dve", "tensor_tensor", [pt, cm], w=[pt], out=pt[:, :], in0=pt[:, :], in1=cm[:, j * 512:(j + 1) * 512], op=ALU.mult)
            for qt in range(4):
                last = 4 * g + qt
                if kt > last:
                    continue
                P.op("pe", lambda qt=qt, last=last: nc.tensor.matmul(pacc[:, qt * 65:(qt + 1) * 65], lhsT=pt[:, qt * 128:(qt + 1) * 128],
                                                                     rhs=Va[:, kt, :], start=(kt == 0 and qt == 0), stop=(kt == last)),
                     r=[pt, Va], w=[pacc] if (kt == 0 and qt == 0) else (), wp=() if (kt == 0 and qt == 0) else [pacc])
            if kt == 4 * g + 3:
                V("dve", "tensor_copy", [pacc], w=[osb], out=osb[:, :, :], in_=pacc[:, 0:260].rearrange("p (t d) -> p t d", d=65))
                V("dve", "reciprocal", [osb], w=[rden], out=rden[:, :, :], in_=osb[:, :, 64:65])
                y_ = yt[g % 2]
                for qt in range(4):
                    V("pool", "tensor_scalar", [osb, rden], wp=[y_], out=y_[:, qt, :], in0=osb[:, qt, 0:64], scalar1=rden[:, qt, :], scalar2=None, op0=ALU.mult)
                P.dma(ya, ya[g * 512:(g + 1) * 512, h * 64:(h + 1) * 64].rearrange("(t p) d -> p t d", p=128), y_, y_[:, :, :], q="pool", part=True)

        for g in range(S // 512):
            for kt in range(4 * g + 4):
                inflight.append(issue0(g, kt))
                if len(inflight) > 2:
                    finish0(*inflight.pop(0))
            if g % 2 == 1 and lru_steps:
                lru_step(*lru_steps.pop(0))
        while inflight:
            finish0(*inflight.pop(0))
        while lru_steps:
            lru_step(*lru_steps.pop(0))
    P.emit()


S = 8192
TC = 2048
NEGB = -30000.0
NH = 8


def mx1_phase(P, io):
    nc = P.nc

    def V(eng, name, r, w=(), wp=(), **kw):
        e = P._eng(eng)
        P.op(eng, lambda: getattr(e, name)(**kw), r=r, w=w, wp=wp)

    u1F, t1F, posd, w1d, w2d, ropeC, ropeS, OHd, cmd, wld, cmkd, ovd, selbd, identd, yd = (io[k] for k in (
        "u1F", "t1F", "cpos", "cw1", "cw2", "ropeC", "ropeS", "OH", "cm", "wl", "cmask", "ov", "selbase", "ident", "y1F"))

    ident_f = P.sb("identf", [128, 128], F32)
    cm = P.sb("cm", [128, 4 * 512], BF16)
    wl = P.sb("wl", [128, 4 * 512], BF16)
    cmask = P.sb("cmask", [128, 5 * 512], BF16)
    ov = P.sb("ov", [128, 4 * 128], BF16)
    for (t, d) in ((ident_f, identd), (cm, cmd), (wl, wld), (cmask, cmkd), (ov, ovd)):
        P.dma(t, t[:, :], d, d[:, :])

    ksT = P.sb("ksT", [128, S], BF16)
    P.dma(ksT, ksT[64:128, :], OHd, OHd[:, :], part=True)
    kwT = P.sb("kwT", [64, S], BF16)
    vsa = P.sb("vsa", [128, 64, 65], BF16)
    vwa = P.sb("vwa", [128, 64, 65], BF16)
    kcmpT = P.sb("kcmpT", [64, 512], BF16)
    vcmp = P.sb("vcmp", [128, 4, 65], BF16)
    st = [P.sb(f"st{i}", [64, TC + 16], F32) for i in range(2)]
    sw = P.sb("sw", [16, 4, 512], F32)
    Cg = P.sb("Cg", [16, 512], F32)
    Sg = P.sb("Sg", [16, 512], F32)
    vst = [P.sb(f"vst{i}", [128, 16, 64], F32) for i in range(2)]
    w1s = P.sb("w1s", [64, 32, 128], F32)
    w2s = P.sb("w2s", [128, 64], F32)
    posT = P.sb("posT", [64, 32], F32)
    hid = P.sb("hid", [128, 512], F32)
    hz = P.sb("hz", [128, 512], F32)
    pbias = P.sb("pbias", [128, 1], F32)

    ps = [P.ps(f"ps{i}", [128, 512]) for i in range(3)]
    po_c = P.ps("po_c", [128, 512])
    pimp = P.ps("pimp", [128, 512])
    po_s = P.ps("po_s", [128, 512])
    po_w = P.ps("po_w", [128, 512])
    pmisc = P.ps("pmisc", [128, 512])
    nst = 0

    V("pool", "memset", [], w=[vsa], ap=vsa[:, :, 64:65], constant=1.0)
    V("pool", "memset", [], w=[vwa], ap=vwa[:, :, 64:65], constant=1.0)
    V("pool", "memset", [], w=[vcmp], ap=vcmp[:, :, 64:65], constant=1.0)
    nv = 0
    for (vcol, dst) in ((0, vsa), (64, vwa)):
        for vc in range(4):
            vs_ = vst[nv % 2]
            nv += 1
            P.dma(vs_, vs_[:, :, :], t1F, t1F[vc * 2048:(vc + 1) * 2048, vcol:vcol + 64].rearrange("(t p) d -> p t d", p=128))
            V("pool", "tensor_copy", [vs_], wp=[dst], out=dst[:, vc * 16:(vc + 1) * 16, 0:64], in_=vs_[:, :, :])

    for (rb, dstA) in ((640, ksT), (704, kwT)):
        for tcn in range(S // TC):
            s_ = st[nst % 2]
            nst += 1
            t0 = tcn * TC
            P.dma(s_, s_[:, 0:TC], u1F, u1F[rb:rb + 64, t0:t0 + TC])
            swv = sw[:, :, :].rearrange("p a b -> p (a b)")
            P.dma(sw, swv[0:8, :], u1F, u1F[rb + 8:rb + 16, t0:t0 + TC], part=True)
            P.dma(sw, swv[8:16, :], u1F, u1F[rb:rb + 8, t0:t0 + TC], part=True)
            for sub in range(4):
                c0 = sub * 512
                P.dma(Cg, Cg[:, :], ropeC, ropeC[:, t0 + c0:t0 + c0 + 512])
                P.dma(Sg, Sg[:, :], ropeS, ropeS[:, t0 + c0:t0 + c0 + 512])
                V("pool", "tensor_tensor", [sw, Sg], w=[sw], out=sw[:, sub, :], in0=sw[:, sub, :], in1=Sg[:, :], op=ALU.mult)
                V("dve", "tensor_tensor", [s_, Cg], w=[s_], out=s_[0:16, c0:c0 + 512], in0=s_[0:16, c0:c0 + 512], in1=Cg[:, :], op=ALU.mult)
                V("dve", "tensor_tensor", [s_, sw], w=[s_], out=s_[0:16, c0:c0 + 512], in0=s_[0:16, c0:c0 + 512], in1=sw[:, sub, :], op=ALU.add)
            V("act", "copy", [s_], wp=[dstA], out=dstA[0:64, t0:t0 + TC], in_=s_[:, 0:TC])

    def gelu_inplace(z, tmp):
        V("pool", "tensor_tensor", [z], w=[tmp], out=tmp[:, :], in0=z[:, :], in1=z[:, :], op=ALU.mult)
        V("pool", "tensor_scalar", [tmp], w=[tmp], out=tmp[:, :], in0=tmp[:, :], scalar1=0.044715, scalar2=1.0, op0=ALU.mult, op1=ALU.add)
        V("pool", "tensor_tensor", [tmp, z], w=[tmp], out=tmp[:, :], in0=tmp[:, :], in1=z[:, :], op=ALU.mult)
        V("act", "activation", [tmp], w=[tmp], out=tmp[:, :], in_=tmp[:, :], func=AF.Sigmoid, scale=1.5957691216057308)
        V("pool", "tensor_tensor", [tmp, z], w=[z], out=z[:, :], in0=tmp[:, :], in1=z[:, :], op=ALU.mult)

    for idx, rb in ((0, 512), (1, 576)):
        P.dma(w1s, w1s[:, :, :], w1d, w1d[idx].rearrange("(j d) m -> d j m", d=64))
        P.dma(w2s, w2s[:, :], w2d, w2d[idx])
        P.dma(posT, posT[:, :], posd, posd[idx].rearrange("j d -> d j"), allow_slow_non_contiguous=True)
        for j in range(32):
            P.op("pe", lambda j=j: nc.tensor.matmul(pmisc[:, 0:1], lhsT=w1s[:, j, :], rhs=posT[:, j:j + 1], start=(j == 0), stop=(j == 31)),
                 r=[w1s, posT], w=[pmisc] if j == 0 else (), wp=[pmisc] if j else ())
        V("dve", "tensor_copy", [pmisc], w=[pbias], out=pbias[:, :], in_=pmisc[:, 0:1])
        V("pool", "memset", [], wp=[hid], ap=hid[:, 511:512], constant=0.0)
        for cc in range(4):
            n = 128 if cc < 3 else 127
            s_ = st[nst % 2]
            nst += 1
            ntok = 2048 + 16 if cc < 3 else 2048
            P.dma(s_, s_[:, 0:ntok], u1F, u1F[rb:rb + 64, cc * 2048:cc * 2048 + ntok])
            sv = s_[:, 0:2048 + 16].rearrange("p (i s) -> p i s", s=16)
            pp = ps[cc % 3]
            for j in range(32):
                P.op("pe", lambda j=j, pp=pp, sv=sv, n=n: nc.tensor.matmul(pp[:, 0:n], lhsT=w1s[:, j, :], rhs=sv[:, (j // 16):(j // 16) + n, j % 16],
                                                                         start=(j == 0), stop=(j == 31)),
                     r=[w1s, s_], w=[pp] if j == 0 else (), wp=[pp] if j else ())
            V("dve", "tensor_scalar", [pp, pbias], wp=[hid], out=hid[:, cc * 128:cc * 128 + n], in0=pp[:, 0:n], scalar1=pbias[:, 0:1], scalar2=None, op0=ALU.add)
        gelu_inplace(hid, hz)
        if idx == 0:
            P.op("pe", lambda: nc.tensor.matmul(pmisc[0:64, :], lhsT=w2s[:, :], rhs=hid[:, :], start=True, stop=True), r=[w2s, hid], w=[pmisc])
            V("act", "copy", [pmisc], w=[kcmpT], out=kcmpT[:, :], in_=pmisc[0:64, :])
        else:
            for cc in range(4):
                P.op("pe", lambda cc=cc: nc.tensor.matmul(pmisc[:, cc * 64:(cc + 1) * 64], lhsT=hid[:, cc * 128:(cc + 1) * 128], rhs=w2s[:, :],
                                                         start=(cc == 0), stop=(cc == 3)), r=[w2s, hid], w=[pmisc] if cc == 0 else (), wp=[pmisc] if cc else ())
            V("act", "copy", [pmisc], wp=[vcmp], out=vcmp[:, :, 0:64], in_=pmisc[:, 0:256].rearrange("p (c d) -> p c d", d=64))

    qst = P.sb("qst", [64, NH, 512], F32)
    qn = P.sb("qn", [64, NH, 512], BF16)
    qa = [P.sb(f"qa{i}", [128, NH, 512], BF16) for i in range(2)]
    qr = qa[0]
    ptile = [P.sb(f"pt{i}", [128, 512], BF16) for i in range(3)]
    oc_sb = [P.sb(f"oc{r}", [128, 4, 65], F32) for r in range(NH)]
    os_sb = P.sb("os_sb", [128, 4, 65], F32)
    ow_sbs = [P.sb(f"ow_sb{r}", [128, 4, 65], F32) for r in range(NH)]
    impacc = P.sb("impacc", [128, 4, 128], F32)
    selb = P.sb("selb", [128, 4, 128], F32)
    tmpi = P.sb("tmpi", [128, 128], F32)
    m8 = P.sb("m8", [128, 16], F32)
    biasT = P.sb("biasT", [128, 512], BF16)
    glt = P.sb("glt", [128, 4, 24], F32)
    cf = P.sb("cf", [128, 3, 4], F32)
    yt = [P.sb(f"yt{i}", [128, 4, NH * 64], F32) for i in range(1)]
    nq = 0

    pipe = []

    def push(mm_list, maskap, fin):
        nonlocal nq
        pp = ps[nq % 3]
        pt = ptile[nq % 3]
        par = nq % 2
        nq += 1
        for i, (l, r_, tl) in enumerate(mm_list):
            P.op("pe", lambda l=l, r_=r_, pp=pp, i=i: nc.tensor.matmul(pp[:, :], lhsT=l, rhs=r_, start=(i == 0), stop=(i == len(mm_list) - 1)),
                 r=tl, w=[pp] if i == 0 else (), wp=[pp] if i else ())
        pipe.append((pp, pt, par, maskap, fin))
        if len(pipe) > 2:
            pop()

    def pop():
        pp, pt, par, maskap, fin = pipe.pop(0)
        V("act", "activation", [pp], w=[pt], out=pt[:, :], in_=pp[:, :], func=AF.Exp, scale=0.125)
        if maskap is not None:
            mt, map_ = maskap
            V("pool" if par else "dve", "tensor_tensor", [pt, mt], w=[pt], out=pt[:, :], in0=pt[:, :], in1=map_, op=ALU.mult)
        fin(pt)

    def flush():
        while pipe:
            pop()

    for g in range(S // 512):
        P.dma(qst, qst[:, :, :], u1F, u1F[0:512, g * 512:(g + 1) * 512].rearrange("(r d) t -> d r t", d=64))
        for hf in range(2):
            P.dma(sw, sw[0:8, :, :], u1F, u1F[0:512, g * 512:(g + 1) * 512].rearrange("(r d) t -> d r t", d=64)[8:16, hf * 4:(hf + 1) * 4, :], part=True)
            P.dma(sw, sw[8:16, :, :], u1F, u1F[0:512, g * 512:(g + 1) * 512].rearrange("(r d) t -> d r t", d=64)[0:8, hf * 4:(hf + 1) * 4, :], part=True)
            if hf == 0:
                P.dma(Cg, Cg[:, :], ropeC, ropeC[:, g * 512:(g + 1) * 512])
                P.dma(Sg, Sg[:, :], ropeS, ropeS[:, g * 512:(g + 1) * 512])
                V("act", "copy", [qst], w=[qn], out=qn[:, :, :], in_=qst[:, :, :])
            for r4 in range(4):
                r = hf * 4 + r4
                V("pool", "tensor_tensor", [sw, Sg], w=[sw], out=sw[:, r4, :], in0=sw[:, r4, :], in1=Sg[:, :], op=ALU.mult)
                V("dve", "tensor_tensor", [qst, Cg], w=[qst], out=qst[0:16, r, :], in0=qst[0:16, r, :], in1=Cg[:, :], op=ALU.mult)
                V("dve", "tensor_tensor", [qst, sw], w=[qst], out=qst[0:16, r, :], in0=qst[0:16, r, :], in1=sw[:, r4, :], op=ALU.add)
        V("act", "copy", [qst], wp=[qa[0]], out=qa[0][0:64, :, :], in_=qst[:, :, :])
        V("pool", "tensor_copy", [qst], wp=[qa[1]], out=qa[1][0:64, :, :], in_=qst[:, :, :])
        P.dma(glt, glt[:, :, :], t1F, t1F[g * 512:(g + 1) * 512, 128:152].rearrange("(t p) c -> p t c", p=128))
        V("act", "activation", [glt], w=[glt], out=glt[:, :, :], in_=glt[:, :, :], func=AF.Sigmoid)
        P.dma(selb, selb[:, :, :], selbd, selbd[g * 512:(g + 1) * 512, :].rearrange("(t p) n -> p t n", p=128))
        ccs = [cc for cc in range(4) if 2048 * cc - 512 * g < 512]
        for r in range(NH):
            for ci, cc in enumerate(ccs):
                off = 2048 * cc - 512 * g
                mk = None
                if off > -2560:
                    m = (off + 2048) // 512
                    mk = (cmask, cmask[:, m * 512:(m + 1) * 512])
                lastc = (ci == len(ccs) - 1)

                def finA(pt, r=r, cc=cc, ci=ci, lastc=lastc, po_c=(po_c if r % 2 == 0 else po_s), pimp=(pimp if r % 2 == 0 else po_w)):
                    for qt in range(4):
                        first = (ci == 0 and qt == 0)
                        P.op("pe", lambda qt=qt, first=first: nc.tensor.matmul(po_c[:, qt * 65:(qt + 1) * 65], lhsT=pt[:, qt * 128:(qt + 1) * 128],
                                                                               rhs=vcmp[:, cc, :], start=first, stop=lastc),
                             r=[pt, vcmp], w=[po_c] if first else (), wp=() if first else [po_c])
                        P.op("pe", lambda qt=qt, first=first: nc.tensor.matmul(pimp[:, qt * 128:(qt + 1) * 128], lhsT=pt[:, qt * 128:(qt + 1) * 128],
                                                                               rhs=ov[:, cc * 128:(cc + 1) * 128], start=first, stop=lastc),
                             r=[pt, ov], w=[pimp] if first else (), wp=() if first else [pimp])
                    if not lastc:
                        return
                    oc = oc_sb[r]
                    V("dve", "tensor_copy", [po_c], w=[oc], out=oc[:, :, :], in_=po_c[:, 0:260].rearrange("p (t d) -> p t d", d=65))
                    V("dve", "tensor_scalar_max", [oc], wp=[cf], out=cf[:, 0, :], in0=oc[:, :, 64], scalar1=1e-30)
                    V("dve", "reciprocal", [cf], w=[cf], out=cf[:, 0, :], in_=cf[:, 0, :])
                    for qt in range(4):
                        if r == 0:
                            V("dve", "tensor_scalar", [pimp, cf], wp=[impacc], out=impacc[:, qt, :], in0=pimp[:, qt * 128:(qt + 1) * 128], scalar1=cf[:, 0, qt:qt + 1],
                              scalar2=None, op0=ALU.mult)
                        else:
                            V("dve", "scalar_tensor_tensor", [pimp, cf, impacc], w=[impacc], out=impacc[:, qt, :], in0=pimp[:, qt * 128:(qt + 1) * 128],
                              scalar=cf[:, 0, qt:qt + 1], in1=impacc[:, qt, :], op0=ALU.mult, op1=ALU.add)
                    V("dve", "tensor_tensor", [cf, glt], w=[oc], out=oc[:, :, 64], in0=cf[:, 0, :], in1=glt[:, :, 3 * r + 0], op=ALU.mult)

                push([(kcmpT[:, cc * 128:(cc + 1) * 128], qn[:, r, :], [kcmpT, qn])], mk, finA)
        for r in range(NH):
            kts = list(range(max(0, 4 * g - 4), 4 * g + 4))
            for kt in kts:
                j = kt - 4 * g
                mk = (cm, cm[:, j * 512:(j + 1) * 512]) if j >= 0 else (wl, wl[:, (j + 4) * 512:(j + 5) * 512])

                def finW(pt, r=r, kt=kt, j=j, firstkt=(kt == kts[0]), lastkt=(kt == kts[-1])):
                    firstw = firstkt
                    for qt in range(4):
                        if not (qt - 4 <= j <= qt):
                            continue
                        P.op("pe", lambda qt=qt, firstw=firstw: nc.tensor.matmul(po_w[:, qt * 65:(qt + 1) * 65], lhsT=pt[:, qt * 128:(qt + 1) * 128],
                                                                                 rhs=vwa[:, kt, :], start=firstw, stop=(j == qt)),
                             r=[pt, vwa], w=[po_w] if firstw else (), wp=() if firstw else [po_w])
                        firstw = False
                    if lastkt:
                        V("dve", "tensor_copy", [po_w], w=[ow_sbs[r]], out=ow_sbs[r][:, :, :], in_=po_w[:, 0:260].rearrange("p (t d) -> p t d", d=65))

                push([(kwT[:, kt * 128:(kt + 1) * 128], qa[0][0:64, r, :], [kwT, qa[0]])], mk, finW)
        flush()
        for qt in range(4):
            V("dve", "tensor_tensor", [impacc, selb], w=[impacc], out=impacc[:, qt, :], in0=impacc[:, qt, :], in1=selb[:, qt, :], op=ALU.add)
            V("dve", "max", [impacc], w=[m8], out=m8[:, 0:8], in_=impacc[:, qt, :])
            V("dve", "match_replace", [impacc, m8], w=[tmpi], out=tmpi[:, :], in_to_replace=m8[:, 0:8], in_values=impacc[:, qt, :], imm_value=-3e38)
            V("dve", "max", [tmpi], w=[m8], out=m8[:, 8:16], in_=tmpi[:, :])
            V("dve", "tensor_scalar_max", [m8], w=[m8], out=m8[:, 15:16], in0=m8[:, 15:16], scalar1=-1e29)
            V("dve", "tensor_scalar", [impacc, m8], w=[tmpi], out=tmpi[:, :], in0=impacc[:, qt, :], scalar1=m8[:, 15:16], scalar2=None, op0=ALU.is_ge)
            V("dve", "tensor_scalar", [tmpi], w=[tmpi], out=tmpi[:, :], in0=tmpi[:, :], scalar1=-NEGB, scalar2=NEGB, op0=ALU.mult, op1=ALU.add)
            P.op("pe", lambda: nc.tensor.transpose(pmisc[:, 0:128], tmpi[:, :], ident_f[:, :]), r=[tmpi, ident_f], w=[pmisc])
            V("act", "copy", [pmisc], wp=[biasT], out=biasT[:, qt * 128:(qt + 1) * 128], in_=pmisc[:, 0:128])
        for r in range(NH):
            V("act" if r % 2 else "pool", "copy" if r % 2 else "tensor_copy", [biasT], wp=[qa[0]], out=qa[0][64:128, r, :], in_=biasT[0:64, :])
            V("pool" if r % 2 else "act", "tensor_copy" if r % 2 else "copy", [biasT], wp=[qa[1]], out=qa[1][64:128, r, :], in_=biasT[64:128, :])
        y_ = yt[0]
        for r in range(NH):
            nkt = 4 * g + 4
            for kt in range(nkt):
                j = kt - 4 * g
                mk = (cm, cm[:, j * 512:(j + 1) * 512]) if j >= 0 else None

                def finS(pt, r=r, kt=kt, nkt=nkt):
                    for qt in range(4):
                        last = 4 * g + qt
                        if kt > last:
                            continue
                        first = (kt == 0 and qt == 0)
                        P.op("pe", lambda qt=qt, first=first, last=last: nc.tensor.matmul(po_s[:, qt * 65:(qt + 1) * 65], lhsT=pt[:, qt * 128:(qt + 1) * 128],
                                                                                         rhs=vsa[:, kt, :], start=first, stop=(kt == last)),
                             r=[pt, vsa], w=[po_s] if first else (), wp=() if first else [po_s])
                    if kt == nkt - 1:
                        V("dve", "tensor_copy", [po_s], w=[os_sb], out=os_sb[:, :, :], in_=po_s[:, 0:260].rearrange("p (t d) -> p t d", d=65))
                        oc = oc_sb[r]
                        ow_sb = ow_sbs[r]
                        for bi, osb_ in ((1, os_sb), (2, ow_sb)):
                            V("dve", "tensor_scalar_max", [osb_], wp=[cf], out=cf[:, bi, :], in0=osb_[:, :, 64], scalar1=1e-30)
                            V("dve", "reciprocal", [cf], w=[cf], out=cf[:, bi, :], in_=cf[:, bi, :])
                            V("dve", "tensor_tensor", [cf, glt], w=[cf], out=cf[:, bi, :], in0=cf[:, bi, :], in1=glt[:, :, 3 * r + bi], op=ALU.mult)
                        for qt in range(4):
                            yv = y_[:, qt, r * 64:(r + 1) * 64]
                            V("dve", "tensor_scalar", [oc], wp=[y_], out=yv, in0=oc[:, qt, 0:64], scalar1=oc[:, qt, 64:65], scalar2=None, op0=ALU.mult)
                            V("dve", "scalar_tensor_tensor", [os_sb, cf, y_], w=[y_], out=yv, in0=os_sb[:, qt, 0:64], scalar=cf[:, 1, qt:qt + 1], in1=yv, op0=ALU.mult, op1=ALU.add)
                            V("dve", "scalar_tensor_tensor", [ow_sb, cf, y_], w=[y_], out=yv, in0=ow_sb[:, qt, 0:64], scalar=cf[:, 2, qt:qt + 1], in1=yv, op0=ALU.mult, op1=ALU.add)

                push([(ksT[:, kt * 128:(kt + 1) * 128], qa[kt // 32][:, r, :], [ksT, qa[kt // 32]])], mk, finS)
        flush()
        P.dma(yd, yd[g * 512:(g + 1) * 512, :].rearrange("(t p) d -> p t d", p=128), y_, y_[:, :, :], q="pool", part=True)
    P.emit()


def mx0_consts():
    half = 8
    freqs = 500000.0 ** (-np.arange(half, dtype=np.float32) * 2.0 / 16)
    ang = np.arange(S, dtype=np.float32)[:, None] * freqs[None, :]
    cos = np.cos(ang).astype(np.float32).T
    sin = np.sin(ang).astype(np.float32).T
    ropeC = np.concatenate([cos, cos], 0)
    ropeS = np.concatenate([-sin, sin], 0)
    onehot = (np.arange(S)[None, :] // 256 == np.arange(32)[:, None]).astype(ml_dtypes.bfloat16)
    p = np.arange(128)[:, None]
    f = np.arange(512)[None, :]
    cm = np.concatenate([((128 * j + p) <= f) for j in range(4)], axis=1).astype(ml_dtypes.bfloat16)
    qb = np.arange(32)[:, None]
    n = np.arange(32)[None, :]
    past = np.where(n < qb, 0.0, -1e30).astype(np.float32).reshape(1, -1).repeat(128, 0)
    own = (n == qb).astype(np.float32).reshape(1, -1).repeat(128, 0)
    return dict(ropeC=np.ascontiguousarray(ropeC), ropeS=np.ascontiguousarray(ropeS), onehot=onehot, cm=np.ascontiguousarray(cm),
                past=np.ascontiguousarray(past), own=np.ascontiguousarray(own), ident=np.eye(128, dtype=np.float32))


def mx1_consts():
    half = 8
    freqs = 500000.0 ** (-np.arange(half, dtype=np.float32) * 2.0 / 16)
    ang = np.arange(S, dtype=np.float32)[:, None] * freqs[None, :]
    cos = np.cos(ang).astype(np.float32).T
    sin = np.sin(ang).astype(np.float32).T
    bf = ml_dtypes.bfloat16
    p = np.arange(128)[:, None]; f = np.arange(512)[None, :]
    cm = np.concatenate([((128 * j + p) <= f) for j in range(4)], axis=1).astype(bf)
    wl = np.concatenate([(f < (128 * jj + p)) for jj in range(4)], axis=1).astype(bf)
    cmask = np.concatenate([(f >= 16 * p + 31 + (-2048 + 512 * m)) for m in range(5)], axis=1).astype(bf)
    OH = ((np.arange(S)[None, :] // 64) % 64 == np.arange(64)[:, None]).astype(bf)
    ncmp = 511
    ci = np.arange(ncmp)[:, None] * 16; sj = np.arange(128)[None, :] * 64
    overlap = np.clip(np.minimum(ci + 32, sj + 64) - np.maximum(ci, sj), 0, None).astype(np.float32) / 32
    ovp = np.zeros((512, 128), np.float32); ovp[:511] = overlap
    ov = ovp.reshape(4, 128, 128).transpose(1, 0, 2).reshape(128, 512).astype(bf)
    t = np.arange(S)[:, None]; jn = np.arange(128)[None, :]
    blk = t // 64
    forced = (jn == 0) | (jn == blk) | (jn == blk - 1)
    selbase = np.where(jn > blk, -1e30, np.where(forced, 1e30, 0.0)).astype(np.float32)
    return dict(ropeC=np.ascontiguousarray(np.concatenate([cos, cos], 0)), ropeS=np.ascontiguousarray(np.concatenate([-sin, sin], 0)),
                OH=OH, cm=np.ascontiguousarray(cm), wl=np.ascontiguousarray(wl), cmask=np.ascontiguousarray(cmask), ov=np.ascontiguousarray(ov),
                selbase=np.ascontiguousarray(selbase), ident=np.eye(128, dtype=np.float32))


PAIRS = [[0, 1], [2, 3], [4, 5], [6, 7]]


def sel_phase(P, sel_d, fm_items, tm_items, gathers):
    nc = P.nc
    for (i_t, i_ap, o_t) in gathers:
        P.allgather(i_t, o_t, PAIRS, in_ap=i_ap)
    sel = P.sb("sel", [128, 2], F32)
    P.dma(sel, sel[:, :], sel_d, sel_d[:, :])
    bufs = [(P.sb(f"sa{i}", [128, 2048], F32), P.sb(f"sb{i}", [128, 2048], F32)) for i in range(3)]
    k = 0

    def blend(ta, tb, apa, apb, wa, wb, rn=128):
        P.op("act", lambda: nc.scalar.activation(out=apa, in_=apa, func=AF.Copy, scale=sel[0:rn, wa:wa + 1]), r=[ta, sel], w=[ta])
        P.op("dve", lambda: nc.vector.scalar_tensor_tensor(out=apa, in0=apb, scalar=sel[0:rn, wb:wb + 1], in1=apa, op0=ALU.mult, op1=ALU.add),
             r=[ta, tb, sel], w=[ta])

    for (dst, dr, dc, A, ar, ac, B, br, bc, nr, ncol, wa, wb) in fm_items:
        for r0 in range(0, nr, 128):
            rn = min(128, nr - r0)
            for c0 in range(0, ncol, 2048):
                cw = min(2048, ncol - c0)
                ta, tb = bufs[k % 3]
                k += 1
                P.dma(ta, ta[0:rn, 0:cw], A, A[ar + r0:ar + r0 + rn, ac + c0:ac + c0 + cw])
                P.dma(tb, tb[0:rn, 0:cw], B, B[br + r0:br + r0 + rn, bc + c0:bc + c0 + cw])
                blend(ta, tb, ta[0:rn, 0:cw], tb[0:rn, 0:cw], wa, wb, rn)
                P.dma(dst, dst[dr + r0:dr + r0 + rn, dc + c0:dc + c0 + cw], ta, ta[0:rn, 0:cw], q="pool", part=True)
    for (dst, dr, dc, A, ar, ac, B, br, bc, nr, ncol, wa, wb) in tm_items:
        tt = 1
        while tt * 2 * ncol <= 2048 and (nr // 128) % (tt * 2) == 0:
            tt *= 2
        step = 128 * tt
        assert nr % step == 0
        for r0 in range(0, nr, step):
            ta, tb = bufs[k % 3]
            k += 1
            va = ta[:, 0:tt * ncol].rearrange("p (t d) -> p t d", d=ncol)
            vb = tb[:, 0:tt * ncol].rearrange("p (t d) -> p t d", d=ncol)
            P.dma(ta, va, A, A[ar + r0:ar + r0 + step, ac:ac + ncol].rearrange("(t p) d -> p t d", p=128))
            P.dma(tb, vb, B, B[br + r0:br + r0 + step, bc:bc + ncol].rearrange("(t p) d -> p t d", p=128))
            blend(ta, tb, ta[:, 0:tt * ncol], tb[:, 0:tt * ncol], wa, wb)
            P.dma(dst, dst[dr + r0:dr + r0 + step, dc:dc + ncol].rearrange("(t p) d -> p t d", p=128), ta, va, q="pool", part=True)
    P.emit()


def build_fused(stop=99):
    P = Prog()
    S2 = 2 * NT
    di = {}

    def din(name, shape, dt=F32):
        di[name] = P.dram_in(name, shape, dt)
        return di[name]

    x = din("x", [NT, D]); c = din("c", [128, 8]); ident = din("ident", [128, 128]); sel = din("sel", [128, 2])
    modw = [din(f"modw{l}", [D, 9 * D]) for l in range(2)]
    modb = [din(f"modb{l}", [1, 9 * D]) for l in range(2)]
    ng = [din(f"ng{l}", [3, D]) for l in range(2)]
    w1 = [[din(f"w1_{l}{i}", [D, 2 * DFF]) for i in range(2)] for l in range(2)]
    w2 = [[din(f"w2_{l}{i}", [DFF, D]) for i in range(2)] for l in range(2)]
    win0 = din("win0", [D, 2560]); wout0 = din("wout0", [D, D]); win1 = din("win1", [D, 1840]); wout1 = din("wout1", [D, D])
    fg = din("fg", [1, D])
    for nm, shp, dt in (("lrup", [256, 8], F32), ("wa", [256, 128], F32), ("wx", [256, 128], F32), ("ropeC", [16, S], F32), ("ropeS", [16, S], F32),
                        ("onehot", [32, S], BF16), ("cm", [128, 2048], BF16), ("past", [128, 1024], F32), ("own", [128, 1024], F32),
                        ("cpos", [2, 32, 64], F32), ("cw1", [2, 2048, 128], F32), ("cw2", [2, 128, 64], F32), ("OH", [64, S], BF16),
                        ("wl", [128, 2048], BF16), ("cmask", [128, 2560], BF16), ("ov", [128, 512], BF16), ("selbase", [S, 128], F32)):
        din(nm, shp, dt)
    out = P.dram_out("out", [NT, D], F32)
    tmp = lambda n, shp: P.dram_tmp(n, shp, F32)
    x1, x3, x4, xs = tmp("x1", [NT, D]), tmp("x3", [NT, D]), tmp("x4", [NT, D]), tmp("xs", [NT, D])
    uA, uB = tmp("uA", [1024, NT]), tmp("uB", [1024, NT])
    gu = [tmp(f"gu{j}", [256, NT]) for j in range(8)]
    vA, vB = tmp("vA", [NT, 256]), tmp("vB", [NT, 256])
    gv = [tmp(f"gv{j}", [4096, 256]) for j in range(2)]
    uF, vF = tmp("uF", [1024, S2]), tmp("vF", [S2, 256])
    ylF, yaF = [tmp(f"ylF{j}", [256, 2048]) for j in range(4)], tmp("yaF", [S2, 256])
    gyl = [tmp(f"gyl{j}", [512, 2048]) for j in range(4)]
    gya = [tmp(f"gya{j}", [4096, 256]) for j in range(4)]
    yTs, ytms = tmp("yTs", [512, NT]), tmp("ytms", [NT, 512])
    u1A, u1B = tmp("u1A", [768, NT]), tmp("u1B", [768, NT])
    gu1 = [tmp(f"gu1{j}", [256, NT]) for j in range(6)]
    t1A, t1B = tmp("t1A", [NT, 152]), tmp("t1B", [NT, 152])
    gt1 = [tmp(f"gt1{j}", [4096, 152]) for j in range(2)]
    u1F, t1F = tmp("u1F", [768, S2]), tmp("t1F", [S2, 152])
    y1F, ytm1s = tmp("y1F", [S2, 512]), tmp("ytm1s", [NT, 1024])
    gy1 = [tmp(f"gy1{j}", [2048, 512]) for j in range(8)]

    base = dict(c=c, ident=ident)
    post1 = dict(norm_idx=1, mod_idx0=3, wc=2560,
                 fm=[(i * 128, uA, i * 128) for i in range(8)] + [(1280 + i * 128, uB, i * 128) for i in range(8)],
                 tm=[(1024, 256, vA, 0), (2304, 256, vB, 0)])
    tp_phase(P, dict(ffn=(0, 0), post=post1), dict(base, x_in=x, x_out=x1, modw=modw[0], modb=modb[0], ng=ng[0], w1=w1[0][0], w2=w2[0][0], win=win0))
    if stop <= 1:
        return P
    sel_phase(P, sel,
              fm_items=[it for j in range(8) for it in ((uF, 128 * j, 0, uA, 128 * j, 0, gu[j], 0, 0, 128, NT, 0, 1),
                                                        (uF, 128 * j, NT, uA, 128 * j, 0, gu[j], 128, 0, 128, NT, 1, 0))],
              tm_items=[it for j in range(2) for it in ((vF, 2048 * j, 0, vA, 2048 * j, 0, gv[j], 0, 0, 2048, 256, 0, 1),
                                                        (vF, NT + 2048 * j, 0, vA, 2048 * j, 0, gv[j], 2048, 0, 2048, 256, 1, 0))],
              gathers=[(uB, uB[128 * j:128 * (j + 1), :], gu[j]) for j in range(8)] + [(vB, vB[2048 * j:2048 * (j + 1), :], gv[j]) for j in range(2)])
    if stop <= 2:
        return P
    mx0_phase(P, dict(di, uF=uF, vF=vF, ylF=ylF, yaF=yaF))
    if stop <= 3:
        return P
    if stop <= 3:
        return P
    if stop <= 4:
        return P
    tp_phase(P, dict(pre=dict(nfm=4, tmw=512), ffn=(2, 6)),
             dict(base, x_in=x1, x_out=x3, modw=modw[0], modb=modb[0], ng=ng[0], w1=w1[0][1], w2=w2[0][1], wout=wout0, xs=xs, sel=sel,
                  gathers=[(ylF[j], None, gyl[j]) for j in range(4)] + [(yaF, yaF[2048 * j:2048 * (j + 1), :], gya[j]) for j in range(4)],
                  fm_cand=lambda tok0: (gyl[tok0 // 2048], gyl[tok0 // 2048][:, tok0 % 2048:tok0 % 2048 + 128],
                                        gyl[2 + tok0 // 2048], gyl[2 + tok0 // 2048][:, tok0 % 2048:tok0 % 2048 + 128]),
                  tm_cand=lambda tok0: [(256 * r, 256, gya[tok0 // 2048], gya[tok0 // 2048][2048 * r + tok0 % 2048:2048 * r + tok0 % 2048 + 128, :],
                                         gya[2 + tok0 // 2048], gya[2 + tok0 // 2048][2048 * r + tok0 % 2048:2048 * r + tok0 % 2048 + 128, :]) for r in range(2)]))
    if stop <= 5:
        return P
    post3 = dict(norm_idx=1, mod_idx0=3, wc=1840,
                 fm=[(i * 128, u1A, i * 128) for i in range(6)] + [(920 + i * 128, u1B, i * 128) for i in range(6)],
                 tm=[(768, 152, t1A, 0), (920 + 768, 152, t1B, 0)])
    tp_phase(P, dict(ffn=(0, 0), post=post3), dict(base, x_in=x3, x_out=x4, modw=modw[1], modb=modb[1], ng=ng[1], w1=w1[1][0], w2=w2[1][0], win=win1))
    if stop <= 6:
        return P
    sel_phase(P, sel,
              fm_items=[it for j in range(6) for it in ((u1F, 128 * j, 0, u1A, 128 * j, 0, gu1[j], 0, 0, 128, NT, 0, 1),
                                                        (u1F, 128 * j, NT, u1A, 128 * j, 0, gu1[j], 128, 0, 128, NT, 1, 0))],
              tm_items=[it for j in range(2) for it in ((t1F, 2048 * j, 0, t1A, 2048 * j, 0, gt1[j], 0, 0, 2048, 152, 0, 1),
                                                        (t1F, NT + 2048 * j, 0, t1A, 2048 * j, 0, gt1[j], 2048, 0, 2048, 152, 1, 0))],
              gathers=[(u1B, u1B[128 * j:128 * (j + 1), :], gu1[j]) for j in range(6)] + [(t1B, t1B[2048 * j:2048 * (j + 1), :], gt1[j]) for j in range(2)])
    if stop <= 7:
        return P
    mx1_phase(P, dict(di, u1F=u1F, t1F=t1F, y1F=y1F))
    if stop <= 8:
        return P
    if stop <= 8:
        return P
    if stop <= 9:
        return P
    tp_phase(P, dict(pre=dict(nfm=0, tmw=1024), ffn=(2, 6), final=True),
             dict(base, x_in=x4, x_out=out, modw=modw[1], modb=modb[1], ng=ng[1], w1=w1[1][1], w2=w2[1][1], wout=wout1, xs=xs, fg=fg, sel=sel,
                  gathers=[(y1F, y1F[1024 * j:1024 * (j + 1), :], gy1[j]) for j in range(8)],
                  tm_cand=lambda tok0: [(512 * r, 512, gy1[tok0 // 1024], gy1[tok0 // 1024][1024 * r + tok0 % 1024:1024 * r + tok0 % 1024 + 128, :],
                                         gy1[4 + tok0 // 1024], gy1[4 + tok0 // 1024][1024 * r + tok0 % 1024:1024 * r + tok0 % 1024 + 128, :]) for r in range(2)]))
    P.nc.sync.nop() if False else None
    return P


_FUSED = {}


def kernel(x, c, mod_w, mod_b, norm_g, ffn_w1, ffn_w2, mix0_in_w, lru_conv_w, lru_conv_b, lru_wa, lru_ba, lru_wx, lru_bx,
           lru_lambda, mix0_out_w, mix1_in_w, cmp_pos, cmp_w1, cmp_w2, mix1_out_w, final_norm_g):
    f32 = np.float32
    A = lambda a: np.ascontiguousarray(np.asarray(a, dtype=f32))
    x, c, mod_w, mod_b, norm_g, ffn_w1, ffn_w2 = map(A, (x, c, mod_w, mod_b, norm_g, ffn_w1, ffn_w2))
    mix0_in_w, mix0_out_w, mix1_in_w, mix1_out_w, final_norm_g = map(A, (mix0_in_w, mix0_out_w, mix1_in_w, mix1_out_w, final_norm_g))
    lru_conv_w, lru_conv_b, lru_wa, lru_ba, lru_wx, lru_bx, lru_lambda = map(A, (lru_conv_w, lru_conv_b, lru_wa, lru_ba, lru_wx, lru_bx, lru_lambda))
    cmp_pos, cmp_w1, cmp_w2 = map(A, (cmp_pos, cmp_w1, cmp_w2))
    if "P" not in _FUSED:
        _FUSED["P"] = build_fused()
    P = _FUSED["P"]
    consts = dict(mx0_consts())
    consts.update(mx1_consts())

    def half0(h):
        return np.concatenate([np.arange(256 * h, 256 * h + 256), 512 + np.arange(256 * h, 256 * h + 256), 1024 + np.arange(256 * h, 256 * h + 256),
                               1536 + np.arange(256 * h, 256 * h + 256), 2048 + np.arange(256 * h, 256 * h + 256)])

    def half1(g):
        r = lambda a, n: a + np.arange(n)
        return np.concatenate([r(512 * g, 512), r(1024 + 64 * g, 64), r(1152 + 64 * g, 64), r(1280 + 64 * g, 64), r(1536 + 64 * g, 64),
                               r(1408 + 64 * g, 64), r(1664 + 64 * g, 64), r(1792 + 24 * g, 24)])

    def bd(w, hh):
        m = np.zeros((256, 128), f32)
        for cc in range(2):
            for nl in range(2):
                n = hh * 4 + cc * 2 + nl
                m[cc * 128 + nl * 64:cc * 128 + nl * 64 + 64, nl * 64:nl * 64 + 64] = w[n]
        return m

    maps = []
    for core in range(8):
        b, p = core // 2, core % 2
        sl = slice(256 * p, 256 * p + 256)
        lrup = np.stack([lru_conv_w[0][0, sl], lru_conv_w[0][1, sl], lru_conv_w[0][2, sl], lru_conv_w[0][3, sl], lru_conv_b[0][sl],
                         lru_ba[0].reshape(-1)[sl], lru_bx[0].reshape(-1)[sl], lru_lambda[0][sl]], axis=1)
        selv = np.zeros((128, 2), f32)
        selv[:, p] = 1.0
        d = dict(x=np.ascontiguousarray(x[b, p * NT:(p + 1) * NT]), c=np.ascontiguousarray(c[b].reshape(8, 128).T), sel=selv,
                 modw0=mod_w[0], modw1=mod_w[1], modb0=mod_b[0:1], modb1=mod_b[1:2], ng0=norm_g[0], ng1=norm_g[1],
                 w1_00=ffn_w1[0, 0], w1_01=ffn_w1[0, 1], w1_10=ffn_w1[1, 0], w1_11=ffn_w1[1, 1],
                 w2_00=ffn_w2[0, 0], w2_01=ffn_w2[0, 1], w2_10=ffn_w2[1, 0], w2_11=ffn_w2[1, 1],
                 win0=np.ascontiguousarray(mix0_in_w[0][:, np.concatenate([half0(p), half0(1 - p)])]), wout0=mix0_out_w[0],
                 win1=np.ascontiguousarray(mix1_in_w[0][:, np.concatenate([half1(p), half1(1 - p)])]), wout1=mix1_out_w[0],
                 fg=final_norm_g.reshape(1, -1), lrup=np.ascontiguousarray(lrup.astype(f32)), wa=bd(lru_wa[0], p), wx=bd(lru_wx[0], p),
                 cpos=cmp_pos[0], cw1=cmp_w1[0], cw2=cmp_w2[0])
        d.update(consts)
        maps.append(d)
    res = run_bass_kernel_spmd(P.nc, maps, core_ids=list(range(8)))
    out = np.empty((4, 2 * NT, D), dtype=f32)
    for core in range(8):
        out[core // 2, (core % 2) * NT:(core % 2 + 1) * NT] = res.results[core]["out"]
    return out
```
